# Optimizing a Trainium2 kernel written in Bass

```python
import jax, jax.numpy as jnp
from jax import lax
import numpy as np

D_MODEL = 1024
BATCH = 4
SEQ = 4096
DEPTH = 1
DEC_BATCH = 128
DEC_SEQ = 4
PAST_LEN = 2048
PAGE_SIZE = 128

N_META = 16
GLA_HEADS = 4
GLA_DK = D_MODEL // 8
GLA_DV = D_MODEL // 4
GLA_RANK = 16
GLA_TAU = 16.0
GLA_CHUNK = 64
SB_HEAD_DIM = 64
SB_HEADS = D_MODEL // SB_HEAD_DIM
SB_QBLOCK = 128
SB_BIAS_INIT = -6.0
D_FF = ((8 * D_MODEL // 3 + 255) // 256) * 256
EPS = 1e-6

QK_GLA = GLA_HEADS * GLA_DK
V_GLA = GLA_HEADS * GLA_DV
SB_W = SB_HEADS * SB_HEAD_DIM
SPLITS = (QK_GLA, QK_GLA, V_GLA, GLA_RANK, V_GLA, SB_W, SB_W, SB_W, D_MODEL, D_MODEL)
D_IN = sum(SPLITS)

kernel_name = "hybrid_gla_stickbreak_macaron_step"


def rms_norm(x, g):
    xf = x.astype(jnp.float32)
    y = xf * lax.rsqrt(jnp.mean(xf * xf, axis=-1, keepdims=True) + EPS)
    return (y * g.astype(jnp.float32)).astype(x.dtype)


def swiglu(x, w_gate, w_up, w_down):
    return (jax.nn.silu(x @ w_gate) * (x @ w_up)) @ w_down


def gla_chunked(q, k, v, g, s0, chunk):
    B, L, H, dk = q.shape
    dv = v.shape[-1]
    n = L // chunk
    q, k, v, g = [t.reshape(B, n, chunk, H, t.shape[-1]) for t in (q, k, v, g)]
    b = jnp.cumsum(g, axis=2)
    b_last = b[:, :, -1:]
    q_in = q * jnp.exp(b)
    k_in = k * jnp.exp(-b)
    k_end = k * jnp.exp(b_last - b)
    causal = jnp.tril(jnp.ones((chunk, chunk), dtype=bool))
    a = jnp.where(causal, jnp.einsum('bnthd,bnshd->bnhts', q_in, k_in), 0.0)
    o = jnp.einsum('bnhts,bnshe->bnthe', a, v)
    ds = jnp.einsum('bnshd,bnshe->nbhde', k_end, v)
    decay = jnp.exp(b_last[:, :, 0]).transpose(1, 0, 2, 3)

    def step(s, inp):
        dec, d = inp
        return dec[..., None] * s + d, s

    s_fin, s_in = lax.scan(step, s0, (decay, ds))
    o = o + jnp.einsum('bnthd,nbhde->bnthe', q_in, s_in)
    return o.reshape(B, L, H, dv), s_fin


def gla_branch(q, k, v, g, s0, lead):
    L = q.shape[1]
    outs = []
    s = s0
    if lead > 0:
        o1, s = gla_chunked(q[:, :lead], k[:, :lead], v[:, :lead], g[:, :lead], s, lead)
        outs.append(o1)
    rest = L - lead
    chunk = GLA_CHUNK if rest % GLA_CHUNK == 0 else rest
    o2, s = gla_chunked(q[:, lead:], k[:, lead:], v[:, lead:], g[:, lead:], s, chunk)
    outs.append(o2)
    o = outs[0] if len(outs) == 1 else jnp.concatenate(outs, axis=1)
    return o, s


def sb_attend(q, k, v, bias, q_off):
    B, Lq, H, d = q.shape
    Lk = k.shape[1]
    qb = min(SB_QBLOCK, Lq)
    n_blk = -(-Lq // qb)
    pad = n_blk * qb - Lq
    qp = jnp.pad(q, ((0, 0), (0, pad), (0, 0), (0, 0)))
    qp = qp.reshape(B, n_blk, qb, H, d).transpose(1, 0, 2, 3, 4)
    q_idx = (q_off + jnp.arange(n_blk * qb, dtype=jnp.int32)).reshape(n_blk, qb)
    k_idx = jnp.arange(Lk, dtype=jnp.int32)
    scale = d ** -0.5
    bias_f = bias.astype(jnp.float32)[None, :, None, None]

    def block(args):
        qblk, qi = args
        z = jnp.einsum('bqhd,bkhd->bhqk', qblk, k).astype(jnp.float32) * scale + bias_f
        mask = k_idx[None, :] < qi[:, None]
        l = jnp.where(mask, jax.nn.log_sigmoid(-z), 0.0)
        r = lax.cumsum(l, axis=3, reverse=True) - l
        a = jnp.where(mask, jnp.exp(jax.nn.log_sigmoid(z) + r), 0.0)
        return jnp.einsum('bhqk,bkhd->bqhd', a.astype(v.dtype), v)

    o = lax.map(block, (qp, q_idx))
    return o.transpose(1, 0, 2, 3, 4).reshape(B, n_blk * qb, H, d)[:, :Lq]


def layer(x, s0, past_k, past_v, lead,
          ffn1_pre_g, ffn1_w_gate, ffn1_w_up, ffn1_w_down, ffn1_post_g,
          mix_pre_g, w_in, w_gk2, b_gk, gla_norm_g, sb_bias, w_o_gla, w_o_sb, w_out, mix_post_g,
          ffn2_pre_g, ffn2_w_gate, ffn2_w_up, ffn2_w_down, ffn2_post_g):
    B, L, _ = x.shape
    f32 = jnp.float32
    h = x + 0.5 * rms_norm(swiglu(rms_norm(x, ffn1_pre_g), ffn1_w_gate, ffn1_w_up, ffn1_w_down), ffn1_post_g)
    u = rms_norm(h, mix_pre_g)
    proj = u @ w_in
    offs = []
    acc = 0
    for sz in SPLITS[:-1]:
        acc += sz
        offs.append(acc)
    q_a, k_a, v_a, gk_low, r_a, q_b, k_b, v_b, gate_a, gate_b = jnp.split(proj, offs, axis=-1)
    q_a = q_a.reshape(B, L, GLA_HEADS, GLA_DK).astype(f32) * (GLA_DK ** -0.5)
    k_a = k_a.reshape(B, L, GLA_HEADS, GLA_DK).astype(f32)
    v_a = v_a.reshape(B, L, GLA_HEADS, GLA_DV).astype(f32)
    g = jax.nn.log_sigmoid((gk_low @ w_gk2 + b_gk).astype(f32)) / GLA_TAU
    g = g.reshape(B, L, GLA_HEADS, GLA_DK)
    o_a, s_new = gla_branch(q_a, k_a, v_a, g, s0.astype(f32), lead)
    o_a = rms_norm(o_a, gla_norm_g).reshape(B, L, V_GLA).astype(x.dtype) * jax.nn.silu(r_a)
    q_b = q_b.reshape(B, L, SB_HEADS, SB_HEAD_DIM)
    k_b = k_b.reshape(B, L, SB_HEADS, SB_HEAD_DIM)
    v_b = v_b.reshape(B, L, SB_HEADS, SB_HEAD_DIM)
    if past_k is None:
        keys, vals, q_off = k_b, v_b, 0
    else:
        keys = jnp.concatenate([past_k.astype(k_b.dtype), k_b], axis=1)
        vals = jnp.concatenate([past_v.astype(v_b.dtype), v_b], axis=1)
        q_off = past_k.shape[1]
    o_b = sb_attend(q_b, keys, vals, sb_bias, q_off).reshape(B, L, SB_W)
    m = jax.nn.sigmoid(gate_a) * (o_a @ w_o_gla) + jax.nn.sigmoid(gate_b) * (o_b @ w_o_sb)
    h = h + rms_norm(m @ w_out, mix_post_g)
    y = h + 0.5 * rms_norm(swiglu(rms_norm(h, ffn2_pre_g), ffn2_w_gate, ffn2_w_up, ffn2_w_down), ffn2_post_g)
    return y, k_b, v_b, s_new


def setup_inputs(seed: int = 0) -> dict:
    key = jax.random.key(seed)
    ks = jax.random.split(key, 32)
    n_pages = PAST_LEN // PAGE_SIZE
    n_pool = (DEC_BATCH * n_pages * 5) // 4
    f32 = jnp.float32

    def nrm(k, shape, scale):
        return jax.random.normal(k, shape, f32) * scale

    def gain(k, n):
        return 1.0 + 0.02 * jax.random.normal(k, (DEPTH, n), f32)

    page_table = jax.random.permutation(ks[5], n_pool)[:DEC_BATCH * n_pages]
    page_table = page_table.reshape(DEC_BATCH, n_pages).astype(jnp.int32)
    return {
        "x_prompt": nrm(ks[0], (BATCH, SEQ, D_MODEL), 1.0),
        "x_sample": nrm(ks[1], (DEC_BATCH, DEC_SEQ, D_MODEL), 1.0),
        "cache_k": nrm(ks[2], (DEPTH, n_pool, PAGE_SIZE, SB_HEADS, SB_HEAD_DIM), 1.0),
        "cache_v": nrm(ks[3], (DEPTH, n_pool, PAGE_SIZE, SB_HEADS, SB_HEAD_DIM), 1.0),
        "state_gla": nrm(ks[4], (DEPTH, DEC_BATCH, GLA_HEADS, GLA_DK, GLA_DV), 1.0),
        "page_table": page_table,
        "meta_tokens": nrm(ks[6], (N_META, D_MODEL), 1.0),
        "ffn1_pre_g": gain(ks[7], D_MODEL),
        "ffn1_w_gate": nrm(ks[8], (DEPTH, D_MODEL, D_FF), D_MODEL ** -0.5),
        "ffn1_w_up": nrm(ks[9], (DEPTH, D_MODEL, D_FF), D_MODEL ** -0.5),
        "ffn1_w_down": nrm(ks[10], (DEPTH, D_FF, D_MODEL), D_FF ** -0.5),
        "ffn1_post_g": gain(ks[11], D_MODEL),
        "mix_pre_g": gain(ks[12], D_MODEL),
        "w_in": nrm(ks[13], (DEPTH, D_MODEL, D_IN), D_MODEL ** -0.5),
        "w_gk2": nrm(ks[14], (DEPTH, GLA_RANK, QK_GLA), GLA_RANK ** -0.5),
        "b_gk": nrm(ks[15], (DEPTH, QK_GLA), 0.01),
        "gla_norm_g": gain(ks[16], GLA_DV),
        "sb_bias": SB_BIAS_INIT + 0.1 * jax.random.normal(ks[26], (DEPTH, SB_HEADS), f32),
        "w_o_gla": nrm(ks[17], (DEPTH, V_GLA, D_MODEL), V_GLA ** -0.5),
        "w_o_sb": nrm(ks[18], (DEPTH, SB_W, D_MODEL), SB_W ** -0.5),
        "w_out": nrm(ks[19], (DEPTH, D_MODEL, D_MODEL), D_MODEL ** -0.5),
        "mix_post_g": gain(ks[20], D_MODEL),
        "ffn2_pre_g": gain(ks[21], D_MODEL),
        "ffn2_w_gate": nrm(ks[22], (DEPTH, D_MODEL, D_FF), D_MODEL ** -0.5),
        "ffn2_w_up": nrm(ks[23], (DEPTH, D_MODEL, D_FF), D_MODEL ** -0.5),
        "ffn2_w_down": nrm(ks[24], (DEPTH, D_FF, D_MODEL), D_FF ** -0.5),
        "ffn2_post_g": gain(ks[25], D_MODEL),
    }


def reference(x_prompt, x_sample, cache_k, cache_v, state_gla, page_table, meta_tokens,
              ffn1_pre_g, ffn1_w_gate, ffn1_w_up, ffn1_w_down, ffn1_post_g,
              mix_pre_g, w_in, w_gk2, b_gk, gla_norm_g, sb_bias, w_o_gla, w_o_sb, w_out, mix_post_g,
              ffn2_pre_g, ffn2_w_gate, ffn2_w_up, ffn2_w_down, ffn2_post_g):
    B = x_prompt.shape[0]
    DB = x_sample.shape[0]
    meta = jnp.broadcast_to(meta_tokens.astype(x_prompt.dtype)[None], (B, N_META, x_prompt.shape[-1]))
    xp = jnp.concatenate([meta, x_prompt], axis=1)
    xs = x_sample
    kp_l, vp_l, sp_l, ks_l, vs_l, ss_l = [], [], [], [], [], []
    for l in range(DEPTH):
        w = (ffn1_pre_g[l], ffn1_w_gate[l], ffn1_w_up[l], ffn1_w_down[l], ffn1_post_g[l],
             mix_pre_g[l], w_in[l], w_gk2[l], b_gk[l], gla_norm_g[l], sb_bias[l], w_o_gla[l], w_o_sb[l],
             w_out[l], mix_post_g[l], ffn2_pre_g[l], ffn2_w_gate[l], ffn2_w_up[l], ffn2_w_down[l],
             ffn2_post_g[l])
        s0 = jnp.zeros((B, GLA_HEADS, GLA_DK, GLA_DV), jnp.float32)
        xp, kp, vp, sp = layer(xp, s0, None, None, N_META, *w)
        past_k = cache_k[l][page_table].reshape(DB, -1, SB_HEADS, SB_HEAD_DIM)
        past_v = cache_v[l][page_table].reshape(DB, -1, SB_HEADS, SB_HEAD_DIM)
        xs, ks, vs, ss = layer(xs, state_gla[l], past_k, past_v, 0, *w)
        kp_l.append(kp); vp_l.append(vp); sp_l.append(sp)
        ks_l.append(ks); vs_l.append(vs); ss_l.append(ss)
    y_prompt = xp[:, N_META:]
    y_sample = xs
    k_prompt = jnp.stack(kp_l)
    v_prompt = jnp.stack(vp_l)
    gla_state_prompt = jnp.stack(sp_l)
    k_sample = jnp.stack(ks_l)
    v_sample = jnp.stack(vs_l)
    gla_state_sample = jnp.stack(ss_l)
    return (y_prompt, y_sample, k_prompt, v_prompt, gla_state_prompt, k_sample, v_sample, gla_state_sample)
```

```python
from contextlib import ExitStack
import math
import numpy as np
import concourse.bass as bass
import concourse.mybir as mybir
from concourse.bass_utils import run_bass_kernel_spmd

F32 = mybir.dt.float32
F32R = mybir.dt.float32r
BF16 = mybir.dt.bfloat16
I32 = mybir.dt.int32
AF = mybir.ActivationFunctionType
ALU = mybir.AluOpType
AX = mybir.AxisListType

ENGS = ("pe", "act", "dve", "pool", "sp")
SAME_ENGINE_SYNC = True

D = 1024
DFF = 2816
FC = 22
N_META = 16
EPS = 1e-6
NSEQ = 16
WSLOT = 2816
NW = 4


class Res:
    __slots__ = ("name", "w", "r")

    def __init__(self, name):
        self.name = name
        self.w = {}
        self.r = {}


class TT:
    def __init__(self, h, name):
        self.h = h
        self.res = Res(name)
        self.name = name

    def __getitem__(self, k):
        return self.h[k]


class FV:
    def __init__(self, t):
        self.h = t.h.bitcast(F32) if hasattr(t.h, "bitcast") else t.h[:].bitcast(F32)
        self.res = t.res
        self.name = t.name

    def __getitem__(self, k):
        return self.h[k]


class Ring:
    def __init__(self, tiles):
        self.t = tiles
        self.i = 0

    def next(self):
        t = self.t[self.i % len(self.t)]
        self.i += 1
        return t


class Prog:
    def __init__(self, nc):
        self.nc = nc
        self.es = ExitStack()
        self.streams = {e: [] for e in ENGS}
        self.count = {e: 0 for e in ENGS}
        self.waited = {e: {} for e in ENGS}
        self.dmacount = {}
        self.phase = ""
        self.tags = {e: [] for e in ENGS}

    def sb(self, name, shape, dt=F32):
        h = self.es.enter_context(self.nc.sbuf_tensor("t_" + name, list(shape), dt))
        return TT(h, name)

    def ps(self, name, shape=(128, 512), dt=F32):
        h = self.es.enter_context(self.nc.psum_tensor("t_" + name, list(shape), dt))
        return TT(h, name)

    def op(self, eng, fn, reads=(), writes=(), dma=None):
        reads = [x.res if isinstance(x, (TT, FV)) else x for x in reads]
        writes = [x.res if isinstance(x, (TT, FV)) else x for x in writes]
        deps = {}

        def add(d):
            for k, v in d.items():
                if deps.get(k, 0) < v:
                    deps[k] = v

        for r in reads:
            add(r.w)
        for w in writes:
            add(w.w)
            add(w.r)
        if dma is None:
            self.count[eng] += 1
            ev = (eng, self.count[eng])
        else:
            key = dma.res.name if isinstance(dma, (TT, FV)) else (dma.name if isinstance(dma, Res) else dma)
            key = "d_" + key
            self.dmacount[key] = self.dmacount.get(key, 0) + 16
            ev = (key, self.dmacount[key])
        waits = []
        for k, v in deps.items():
            if k == eng and (eng == "pe" or not SAME_ENGINE_SYNC):
                continue
            if k.startswith("d_") and k != ev[0]:
                v = max(v, self.dmacount.get(k, 0))
            if self.waited[eng].get(k, 0) >= v:
                continue
            self.waited[eng][k] = v
            waits.append((k, v))
        self.streams[eng].append((waits, fn, ev))
        self.tags[eng].append(self.phase)
        for r in reads:
            if r.r.get(ev[0], 0) < ev[1]:
                r.r[ev[0]] = ev[1]
        for w in writes:
            w.w[ev[0]] = ev[1]
            w.r = {}
        return ev

    def barrier(self, froms, tos):
        for t in tos:
            t = t.res if isinstance(t, (TT, FV)) else t
            for f in froms:
                f = f.res if isinstance(f, (TT, FV)) else f
                for d in (f.w, f.r):
                    for k, v in d.items():
                        if t.r.get(k, 0) < v:
                            t.r[k] = v

    def emit(self):
        nc = self.nc
        es = self.es
        keys = list(ENGS) + sorted(self.dmacount.keys())
        sems = {k: es.enter_context(nc.semaphore("s_" + k)) for k in keys}
        fin = []
        for e in ENGS:
            if e != "sp" and self.count[e] > 0:
                fin.append((e, self.count[e]))
        for k, v in self.dmacount.items():
            fin.append((k, v))
        streams = self.streams
        block = es.enter_context(nc.Block())

        def make(name):
            def body(eng):
                for waits, fn, ev in streams[name]:
                    for k, v in waits:
                        eng.wait_ge(sems[k], v)
                    inst = fn(eng)
                    inst.then_inc(sems[ev[0]], 16 if ev[0].startswith("d_") else 1)
                if name == "sp":
                    for k, v in fin:
                        eng.wait_ge(sems[k], v)
            return body

        block.sync(make("sp"))
        block.scalar(make("act"))
        block.vector(make("dve"))
        block.gpsimd(make("pool"))
        block.tensor(make("pe"))
        es.close()
        return nc

    def dma(self, out, in_, reads=(), writes=(), sem=None, eng="sp"):
        return self.op(eng, lambda e: e.dma_start(out=out, in_=in_), reads=reads, writes=writes, dma=sem)

    def mm(self, out, lhsT, rhs, start, stop, reads, writes):
        return self.op("pe", lambda e: e.matmul(out, lhsT, rhs, start=start, stop=stop), reads=reads, writes=writes)

    def tr(self, out, in_, ident, reads, writes):
        return self.op("pe", lambda e: e.transpose(out, in_, ident), reads=reads, writes=writes)

    def act(self, out, in_, func, reads, writes, bias=None, scale=None):
        kw = {}
        if bias is not None:
            kw["bias"] = bias
        if scale is not None:
            kw["scale"] = scale
        return self.op("act", lambda e: e.activation(out, in_, func, **kw), reads=reads, writes=writes)

    def tt(self, out, in0, in1, op, reads, writes, eng="dve"):
        return self.op(eng, lambda e: e.tensor_tensor(out, in0, in1, op), reads=reads, writes=writes)

    def ts(self, out, in0, s1, op0, reads, writes, s2=None, op1=None, eng="dve"):
        if op1 is None:
            return self.op(eng, lambda e: e.tensor_scalar(out, in0, s1, None, op0), reads=reads, writes=writes)
        return self.op(eng, lambda e: e.tensor_scalar(out, in0, s1, s2, op0, op1), reads=reads, writes=writes)

    def stt(self, out, in0, scalar, in1, op0, op1, reads, writes):
        return self.op("dve", lambda e: e.scalar_tensor_tensor(out, in0, scalar, in1, op0, op1),
                       reads=reads, writes=writes)

    def copy(self, out, in_, reads, writes, eng="dve"):
        if eng == "act":
            return self.op("act", lambda e: e.copy(out, in_), reads=reads, writes=writes)
        return self.op(eng, lambda e: e.tensor_copy(out, in_), reads=reads, writes=writes)

    def memset(self, ap, val, writes, eng="pool"):
        return self.op(eng, lambda e: e.memset(ap, val), writes=writes)


C_ID, C_LE, C_GT, C_LES, C_GTS = 0, 128, 256, 384, 512
C_MD0, C_MD1 = 640, 896
C_MLT = 1152
C_SEQ = 1216
C_VP, C_VS, C_PCOL = 1232, 1233, 1234
C_ONES = 1235
C_EPS, C_LNH, C_ZERO, C_ONE = 1363, 1364, 1365, 1366
C_EVEN, C_ODD = 1367, 1368
C_HM = 1369
C_TOT = C_HM + 1024
R_GT, R_ONES = 0, 128
R_TOT = 256
P_F1PRE, P_F1POST, P_MPRE, P_MPOST, P_F2PRE, P_F2POST = 0, 8, 16, 24, 32, 40
P_GLAG, P_SBB, P_SBB64 = 48, 50, 66
P_TOT = 130


def make_consts(rem):
    c = np.zeros((128, C_TOT), np.float32)
    p = np.arange(128)
    c[:, C_ID:C_ID + 128] = np.eye(128)
    c[:, C_LE:C_LE + 128] = (p[:, None] <= p[None, :])
    c[:, C_GT:C_GT + 128] = (p[:, None] > p[None, :])
    same = (p[:, None] // 4 == p[None, :] // 4) & (p[:, None] < 64) & (p[None, :] < 64)
    c[:, C_LES:C_LES + 128] = same & (p[:, None] <= p[None, :])
    c[:, C_GTS:C_GTS + 128] = same & (p[:, None] > p[None, :])
    col = np.arange(256)
    c[:, C_MD0:C_MD0 + 256] = (p[:, None] < col[None, :])
    c[:, C_MD1:C_MD1 + 256] = (p[:, None] + 128 < col[None, :])
    ht = np.arange(64)
    c[:, C_MLT:C_MLT + 64] = ((p[:, None] % 4) < (ht[None, :] % 4))
    c[:, C_SEQ:C_SEQ + 16] = (p[:, None] // 4 == np.arange(16)[None, :]) & (p[:, None] < 64)
    c[:, C_VP] = p < rem
    c[:, C_VS] = p < 64
    c[:, C_PCOL] = p
    c[:, C_ONES:C_ONES + 128] = 1.0
    c[:, C_EPS] = EPS
    c[:, C_LNH] = math.log(0.5)
    c[:, C_ZERO] = 0.0
    c[:, C_ONE] = 1.0
    hh = p // 4
    c[:, C_EVEN] = (hh % 2 == 0) & (p < 64)
    c[:, C_ODD] = (hh % 2 == 1) & (p < 64)
    hm = np.zeros((128, 16, 64), np.float32)
    for q in range(64):
        hm[q, q // 4, :] = 1.0
    c[:, C_HM:C_HM + 1024] = hm.reshape(128, 1024)
    r = np.zeros((128, R_TOT), np.float32)
    r[:, R_GT:R_GT + 128] = c[:, C_GT:C_GT + 128]
    r[:, R_ONES:R_ONES + 128] = 1.0
    return c, r


def slabs(W, ncol):
    K, N = W.shape
    kc = K // 128
    a = W.reshape(kc, 128, N // ncol, ncol).transpose(2, 1, 0, 3)
    return np.ascontiguousarray(a).reshape(N // ncol, 128, kc * ncol)


WNAMES = ["f1g", "f1u", "f1d", "qa", "ka", "va", "gl", "ra", "qb", "kb", "vb", "ga", "gb",
          "wog", "wos", "wout", "f2g", "f2u", "f2d"]


def build(cfg):
    L = cfg["L"]
    NPG = cfg["NPG"]
    NPOOL = cfg["NPOOL"]
    NPR = cfg["NPROMPT"]
    NSG = cfg["NSG"]
    wshapes = cfg["wshapes"]
    n256 = L // 256
    rem = L - 256 * n256
    tiles = [(i * 256, 256) for i in range(n256)]
    if rem:
        assert rem <= 128
        tiles.append((n256 * 256, 128))
    LPAD = tiles[-1][0] + tiles[-1][1]
    assert LPAD == cfg["LPAD"]

    nc = bass.Bass("TRN2", target_bir_lowering=False)

    def din(name, shape, dt=F32):
        return nc.dram_tensor(name, list(shape), dt, kind="ExternalInput").ap()

    def dout(name, shape, dt=F32):
        return nc.dram_tensor(name, list(shape), dt, kind="ExternalOutput").ap()

    xp = din("xp", [NPR * LPAD, D])
    xs = din("xs", [NSG * 128, D])
    ck = din("ck", [NPOOL * 128, D])
    cv = din("cv", [NPOOL * 128, D])
    ptb = din("ptb", [NSG, NSEQ * NPG], I32)
    sg_in = din("sg", [NSG * NSEQ, 128, 1024])
    cst_d = din("cst", [128, C_TOT])
    cstr_d = din("cstr", [128, R_TOT])
    par_d = din("par", [128, P_TOT])
    wgk_d = din("wgk", [16, 512])
    bgk_d = din("bgk", [1, 512])
    WS = {n: din("w_" + n, wshapes[n]) for n in WNAMES}

    yp = dout("yp", [NPR * LPAD, D])
    ys = dout("ys", [NSG * 128, D])
    kp = dout("kp", [NPR * LPAD, D])
    vp = dout("vp", [NPR * LPAD, D])
    sp_out = dout("spo", [NPR * 128, 1024])
    ks = dout("ks", [NSG * 128, D])
    vs = dout("vs", [NSG * 128, D])
    ss_out = dout("sso", [NSG * NSEQ, 128, 1024])
    kts = nc.dram_tensor("kts", [8, 128, LPAD], BF16).ap()

    P = Prog(nc)
    cst = P.sb("cst", [128, C_TOT])
    cstr = P.sb("cstr", [128, R_TOT], F32R)
    par = P.sb("par", [128, P_TOT])
    wgk = P.sb("wgk", [16, 512])
    bgk = P.sb("bgk", [1, 512])
    idx = P.sb("idx", [128, NSEQ * NPG], I32)
    ptt = P.sb("ptt", [128, NSEQ * NPG], I32)
    wring = Ring([P.sb("wr%d" % i, [128, WSLOT], BF16) for i in range(NW)])
    tok = Ring([P.sb("tok%d" % i, [128, D]) for i in range(3)])
    hT = P.sb("hT", [128, 8, 256])
    nT = P.sb("nT", [128, 8, 256], BF16)
    fT = P.sb("fT", [128, 8, 256])
    arena = P.sb("arena", [128, 3072], F32R)
    actT = P.sb("actT", [128, FC, 256], BF16)
    QT = P.sb("QT", [128, 8, 256], BF16)
    KTst = P.sb("KTst", [128, 8, 256], BF16)
    ktr = Ring([P.sb("kt%d" % i, [128, 512], BF16) for i in range(2)])
    vtr = Ring([(P.sb("vta%d" % i, [128, 4, 128], BF16), P.sb("vtb%d" % i, [128, 4, 128], BF16)) for i in range(2)])
    ktp_t = [P.sb("ktp%d" % i, [128, 8, 128], BF16) for i in range(2)]
    qblk_t = P.sb("qblk", [128, 8, 64], BF16)
    wkb = Ring([P.sb("wkb%d" % i, [128, 256], BF16) for i in range(3)])
    sq = Ring([P.sb("sq%d" % i, [128, 256], F32R) for i in range(2)])
    rstd = Ring([P.sb("rstd%d" % i, [128, 256]) for i in range(2)])
    wk = Ring([P.sb("wk%d" % i, [128, 256]) for i in range(6)])
    wkr = Ring([P.sb("wkr%d" % i, [128, 256], F32R) for i in range(4)])
    big = Ring([P.sb("big%d" % i, [128, 512]) for i in range(1)])
    mrg = [P.sb("mrg%d" % i, [128, 256]) for i in range(2)]
    arena2 = P.sb("arena2", [128, 8704], F32R)

    def view(a, n, name, pat=None, **kw):
        ap = arena2[:, a:a + n]
        if pat:
            ap = ap.rearrange(pat, **kw)
        return TT(ap, name)

    qaT = view(0, 1024, "qaT", "p (c t) -> p c t", t=256)
    kaT = view(1024, 1024, "kaT", "p (c t) -> p c t", t=256)
    ka_tok = view(2048, 1024, "ka_tok", "p (b t) -> p b t", t=512)
    va_tok = view(3072, 2048, "va_tok", "p (b t) -> p b t", t=1024)
    sp_tok = view(5120, 1024, "sp_tok", "p (b t) -> p b t", t=512)
    gq = view(6144, 512, "gq", "p (h t) -> p h t", t=128)
    gk_ = view(6656, 512, "gk_", "p (h t) -> p h t", t=128)
    ge = view(7168, 512, "ge", "p (h t) -> p h t", t=128)
    ge1 = view(7680, 512, "ge1", "p (h t) -> p h t", t=128)
    roleA = [qaT, kaT, ka_tok, va_tok, sp_tok, gq, gk_, ge, ge1]
    qaTf, kaTf, ka_tokf, va_tokf, sp_tokf, gqf, gk_f, gef, ge1f = [FV(t) for t in roleA]
    lacc = [view(7168 + 256 * i, 256, "lacc%d" % i) for i in range(2)]
    vtok_s = view(7680, 1024, "vtok_s")
    roleB = lacc + [vtok_s]
    glT = P.sb("glT", [128, 256])
    oaT = P.sb("oaT", [128, 8, 256], BF16)
    obT = P.sb("obT", [128, 8, 256], BF16)
    S = P.sb("S", [128, 1024])
    Ssm = S
    ost = P.sb("ost", [128, 8, 64])
    psr = Ring([P.ps("ps%d" % i) for i in range(6)])
    psO = [P.ps("psO%d" % i) for i in range(2)]

    def c_(a, n=1):
        return cst[:, a:a + n]

    ident = cst[:, C_ID:C_ID + 128]
    ones_r = cstr[:, R_ONES:R_ONES + 128]
    mgt_r = cstr[:, R_GT:R_GT + 128]

    P.dma(cst[:], cst_d, writes=[cst], sem=cst)
    P.dma(cstr[:], cstr_d.bitcast(F32R), writes=[cstr], sem=cstr, eng="pool")
    P.dma(par[:], par_d, writes=[par], sem=par)
    P.dma(wgk[:], wgk_d, writes=[wgk], sem=wgk)
    P.dma(bgk[:], bgk_d, writes=[bgk], sem=bgk)
    for (va_, vb2) in vtr.t:
        zsrc = cst[:, 0:256].rearrange("p (b d) -> p b d", d=64)
        P.ts(va_[:, :, 64:128], zsrc, 0.0, ALU.mult, reads=[cst], writes=[va_], eng="pool")
        P.ts(vb2[:, :, 0:64], zsrc, 0.0, ALU.mult, reads=[cst], writes=[vb2], eng="pool")

    def load_slab(name, s):
        slot = wring.next()
        F = wshapes[name][2]
        nsp = 1 if F <= 2048 else 2
        w_ = F // nsp
        for q in range(nsp):
            P.dma(slot[:, q * w_:(q + 1) * w_], WS[name][s][:, q * w_:(q + 1) * w_], writes=[slot], sem=slot,
                  eng="pool")
        return slot

    ev_i = [0]

    def evac_eng():
        ev_i[0] += 1
        return "act" if ev_i[0] % 2 else "dve"

    def rms_stats(src, chunks, T, dn, lnb):
        ps = psr.next()
        n = len(chunks)
        for i, c in enumerate(chunks):
            s = sq.next()
            P.tt(s[:, 0:T], src[:, c, 0:T], src[:, c, 0:T], ALU.mult, reads=[src], writes=[s], eng="pool")
            P.mm(ps[:, 0:T], ones_r, s[:, 0:T], i == 0, i == n - 1, reads=[cstr, s], writes=[ps])
        lv = wk.next()
        P.act(lv[:, 0:T], ps[:, 0:T], AF.Ln, reads=[ps, cst], writes=[lv], scale=1.0 / dn, bias=c_(C_EPS))
        r = rstd.next()
        P.act(r[:, 0:T], lv[:, 0:T], AF.Exp, reads=[lv, cst], writes=[r], scale=-0.5, bias=c_(lnb))
        return r

    def norm_to_nT(goff, T):
        r = rms_stats(hT, list(range(8)), T, D, C_ZERO)
        for c in range(8):
            P.stt(nT[:, c, 0:T], hT[:, c, 0:T], par[:, goff + c:goff + c + 1], r[:, 0:T], ALU.mult, ALU.mult,
                  reads=[hT, par, r], writes=[nT])

    def resid_add(goff, T, lnb):
        r = rms_stats(fT, list(range(8)), T, D, lnb)
        for c in range(8):
            t = wk.next()
            P.stt(t[:, 0:T], fT[:, c, 0:T], par[:, goff + c:goff + c + 1], r[:, 0:T], ALU.mult, ALU.mult,
                  reads=[fT, par, r], writes=[t])
            P.tt(hT[:, c, 0:T], hT[:, c, 0:T], t[:, 0:T], ALU.add, reads=[hT, t], writes=[hT], eng="pool")

    def fm_group(slot, ncol, j, rhsT, T, KC=8):
        ps = psr.next()
        for kc in range(KC):
            P.mm(ps[:, 0:T], slot[:, kc * ncol + j * 128: kc * ncol + (j + 1) * 128], rhsT[:, kc, 0:T],
                 kc == 0, kc == KC - 1, reads=[slot, rhsT], writes=[ps])
        return ps

    def ffn(pre, post, wg, wu, wd, T):
        norm_to_nT(pre, T)
        aT = actT
        for s in range(FC // 2):
            sg = load_slab(wg, s)
            su = load_slab(wu, s)
            for j in range(2):
                f = 2 * s + j
                pg = fm_group(sg, 256, j, nT, T)
                pu = fm_group(su, 256, j, nT, T)
                sl = wk.next()
                P.act(sl[:, 0:T], pg[:, 0:T], AF.Silu, reads=[pg], writes=[sl])
                P.tt(aT[:, f, 0:T], sl[:, 0:T], pu[:, 0:T], ALU.mult, reads=[sl, pu], writes=[actT])
        for c in range(8):
            sd = load_slab(wd, c)
            pd = psr.next()
            for f in range(FC):
                P.mm(pd[:, 0:T], sd[:, f * 128:(f + 1) * 128], aT[:, f, 0:T], f == 0, f == FC - 1,
                     reads=[sd, actT], writes=[pd])
            P.copy(fT[:, c, 0:T], pd[:, 0:T], reads=[pd], writes=[fT], eng=evac_eng())
        resid_add(post, T, C_LNH)

    def proj_fm(name, T, dst, scale=None, nchunks=None):
        ns = wshapes[name][0]
        for s in range(ns):
            slot = load_slab(name, s)
            for j in range(2):
                c = 2 * s + j
                ps = fm_group(slot, 256, j, nT, T)
                ap, tl = dst(c)
                if scale is None:
                    P.copy(ap, ps[:, 0:T], reads=[ps], writes=[tl], eng=evac_eng())
                else:
                    P.op("act", lambda e, ap=ap, ps=ps: e.mul(ap, ps[:, 0:T], scale), reads=[ps], writes=[tl])

    def proj_tm(name, T, dst):
        ns = wshapes[name][0]
        for s in range(ns):
            slot = load_slab(name, s)
            for b in range(T // 128):
                ps = psr.next()
                for kc in range(8):
                    P.mm(ps[:, 0:256], nT[:, kc, b * 128:(b + 1) * 128], slot[:, kc * 256:(kc + 1) * 256],
                         kc == 0, kc == 7, reads=[nT, slot], writes=[ps])
                ap, tl = dst(b, s)
                P.copy(ap, ps[:, 0:256], reads=[ps], writes=[tl], eng=evac_eng())

    def do_tile(kind, t0, T, last_prompt, qi=0):
        NB = T // 128
        sample = kind == "sample"
        xsrc = xs if sample else xp
        ro = qi * (128 if sample else LPAD)
        so = qi * NSEQ
        if sample:
            P.dma(ptt[:], ptb[qi:qi + 1, :].partition_broadcast(128), writes=[ptt], sem=ptt)
            P.ts(idx[:], ptt[:], 128.0, ALU.mult, reads=[ptt, cst], writes=[idx], s2=c_(C_PCOL), op1=ALU.add)
        elif t0 == 0:
            P.memset(S[:], 0.0, [S])
        P.phase = kind + ":load"
        sts = []
        for b in range(NB):
            st = tok.next()
            P.dma(st[:, :], xsrc[ro + t0 + b * 128: ro + t0 + (b + 1) * 128, :], writes=[st], sem=st)
            sts.append(st)
        for c in range(8):
            ps = psr.next()
            for b in range(NB):
                P.tr(ps[:, b * 128:(b + 1) * 128], sts[b][:, c * 128:(c + 1) * 128], ident,
                     reads=[sts[b], cst], writes=[ps])
            P.copy(hT[:, c, 0:T], ps[:, 0:T], reads=[ps], writes=[hT], eng=evac_eng())
        P.phase = kind + ":ffn1"
        ffn(P_F1PRE, P_F1POST, "f1g", "f1u", "f1d", T)
        P.phase = kind + ":mixnorm"
        norm_to_nT(P_MPRE, T)
        P.phase = kind + ":gla"
        P.barrier(roleB, roleA)
        proj_fm("qa", T, lambda c: (qaT[:, c, 0:T], qaT), scale=128 ** -0.5)
        proj_fm("ka", T, lambda c: (kaT[:, c, 0:T], kaT))
        slot = load_slab("gl", 0)
        ps = fm_group(slot, 128, 0, nT, T)
        P.copy(glT[:, 0:T], ps[:, 0:T], reads=[ps], writes=[glT], eng=evac_eng())
        proj_tm("ka", T, lambda b, s: (ka_tok[:, b, s * 256:(s + 1) * 256], ka_tok))
        proj_tm("va", T, lambda b, s: (va_tok[:, b, s * 256:(s + 1) * 256], va_tok))
        for b in range(NB):
            ps = psr.next()
            P.mm(ps[:, 0:512], glT[0:16, b * 128:(b + 1) * 128], wgk[0:16, :], True, False,
                 reads=[glT, wgk], writes=[ps])
            P.mm(ps[:, 0:512], cst[0:1, C_ONES:C_ONES + 128], bgk[0:1, :], False, True,
                 reads=[cst, bgk], writes=[ps])
            e1 = big.next()
            P.act(e1[:, :], ps[:, 0:512], AF.Exp, reads=[ps], writes=[e1], scale=-1.0)
            if sample or (last_prompt and b == NB - 1):
                e2_ = tok.next()
                P.act(e2_[:, 0:512], e1[:, :], AF.Ln, reads=[e1, cst], writes=[e2_], bias=c_(C_ONE))
                vm = c_(C_VS) if sample else c_(C_VP)
                P.ts(sp_tok[:, b, :], e2_[:, 0:512], vm, ALU.mult, reads=[e2_, cst], writes=[sp_tok])
            else:
                P.act(sp_tok[:, b, :], e1[:, :], AF.Ln, reads=[e1, cst], writes=[sp_tok], bias=c_(C_ONE))
        mle = c_(C_LES, 128) if sample else c_(C_LE, 128)
        mgt = c_(C_GTS, 128) if sample else c_(C_GT, 128)
        for b in range(NB):
            tk = slice(b * 128, (b + 1) * 128)
            for h in range(4):
                spc = sp_tokf[:, b, h * 128:(h + 1) * 128]
                pcs = psr.next()
                P.mm(pcs[:, 0:128], spc, mle, True, True, reads=[sp_tok, cst], writes=[pcs])
                prem = psr.next()
                P.mm(prem[:, 0:128], mgt, spc, True, True, reads=[sp_tok, cst], writes=[prem])
                P.act(ge1[:, h, :], pcs[:, 0:128], AF.Exp, reads=[pcs], writes=[ge1], scale=-1.0 / 16)
                e2 = wk.next()
                P.act(e2[:, 0:128], pcs[:, 0:128], AF.Exp, reads=[pcs], writes=[e2], scale=1.0 / 16)
                e3 = wk.next()
                P.act(e3[:, 0:128], prem[:, 0:128], AF.Exp, reads=[prem], writes=[e3], scale=-1.0 / 16)
                P.tt(gq[:, h, :], qaTf[:, h, tk], ge1f[:, h, :], ALU.mult, reads=[qaT, ge1], writes=[gq])
                P.tt(gk_[:, h, :], kaTf[:, h, tk], e2[:, 0:128], ALU.mult, reads=[kaT, e2], writes=[gk_])
                P.tt(ge[:, h, :], ka_tokf[:, b, h * 128:(h + 1) * 128], e3[:, 0:128], ALU.mult,
                     reads=[ka_tok, e3], writes=[ge])
            if sample:
                for j in range(NSEQ):
                    P.dma(Ssm[:, :], sg_in[so + j], writes=[Ssm], sem=Ssm)
                    pso = psr.next()
                    for h in range(4):
                        for ec in range(2):
                            q = h * 2 + ec
                            P.mm(pso[:, q * 4:q * 4 + 4], Ssm[:, h * 256 + ec * 128: h * 256 + (ec + 1) * 128],
                                 gqf[:, h, 4 * j:4 * j + 4], True, True, reads=[Ssm, gq], writes=[pso])
                    P.copy(ost[:, :, 4 * j:4 * j + 4], pso[:, 0:32].rearrange("p (q t) -> p q t", t=4),
                           reads=[pso], writes=[ost], eng=evac_eng())
                    for h in range(4):
                        km = wk.next()
                        P.ts(km[:, 0:128], gef[:, h, :], cst[:, C_SEQ + j:C_SEQ + j + 1], ALU.mult,
                             reads=[ge, cst], writes=[km])
                        pd = psr.next()
                        P.mm(pd[:, 0:256], km[:, 0:128], va_tokf[:, 0, h * 256:(h + 1) * 256], True, True,
                             reads=[km, va_tok], writes=[pd])
                        P.stt(Ssm[:, h * 256:(h + 1) * 256], Ssm[:, h * 256:(h + 1) * 256],
                              ge1f[:, h, 4 * j + 3:4 * j + 4], pd[:, 0:256], ALU.mult, ALU.add,
                              reads=[Ssm, ge1, pd], writes=[Ssm])
                    P.dma(ss_out[so + j], Ssm[:, :], reads=[Ssm], sem=Ssm)
            for h in range(4):
                pa = psr.next()
                P.mm(pa[:, 0:128], gk_f[:, h, :], gqf[:, h, :], True, True, reads=[gk_, gq], writes=[pa])
                am = wk.next()
                P.tt(am[:, 0:128], pa[:, 0:128], mle, ALU.mult, reads=[pa, cst], writes=[am])
                for ec in range(2):
                    q = h * 2 + ec
                    po = psr.next()
                    P.mm(po[:, 0:128], va_tokf[:, b, h * 256 + ec * 128: h * 256 + (ec + 1) * 128], am[:, 0:128],
                         True, sample, reads=[va_tok, am], writes=[po])
                    if not sample:
                        P.mm(po[:, 0:128], S[:, h * 256 + ec * 128: h * 256 + (ec + 1) * 128], gqf[:, h, :],
                             False, True, reads=[S, gq], writes=[po])
                        P.copy(fT[:, q, tk], po[:, 0:128], reads=[po], writes=[fT], eng=evac_eng())
                    else:
                        P.tt(fT[:, q, 0:64], po[:, 0:64], ost[:, q, :], ALU.add, reads=[po, ost], writes=[fT])
                        P.copy(fT[:, q, 64:128], po[:, 64:128], reads=[po], writes=[fT], eng=evac_eng())
                if not sample:
                    pd = psr.next()
                    P.mm(pd[:, 0:256], gef[:, h, :], va_tokf[:, b, h * 256:(h + 1) * 256], True, True,
                         reads=[ge, va_tok], writes=[pd])
                    P.stt(S[:, h * 256:(h + 1) * 256], S[:, h * 256:(h + 1) * 256], ge1f[:, h, 127:128],
                          pd[:, 0:256], ALU.mult, ALU.add, reads=[S, ge1, pd], writes=[S])
        if last_prompt:
            P.dma(sp_out[qi * 128:(qi + 1) * 128, :], S[:, :], reads=[S], sem=S)
        P.phase = kind + ":glagate"
        rs = []
        for h in range(4):
            rs.append(rms_stats(fT, [2 * h, 2 * h + 1], T, 256, C_ZERO) if h < 2 else None)
        ra_slots = {}

        def gate_chunk(c, r):
            s, j = divmod(c, 2)
            if s not in ra_slots:
                ra_slots.clear()
                ra_slots[s] = load_slab("ra", s)
            ps = fm_group(ra_slots[s], 256, j, nT, T)
            sr = wk.next()
            P.act(sr[:, 0:T], ps[:, 0:T], AF.Silu, reads=[ps], writes=[sr])
            t1 = wk.next()
            P.stt(t1[:, 0:T], fT[:, c, 0:T], par[:, P_GLAG + (c % 2):P_GLAG + (c % 2) + 1], r[:, 0:T],
                  ALU.mult, ALU.mult, reads=[fT, par, r], writes=[t1])
            P.tt(oaT[:, c, 0:T], t1[:, 0:T], sr[:, 0:T], ALU.mult, reads=[t1, sr], writes=[oaT])

        for h in range(2):
            for ec in range(2):
                gate_chunk(2 * h + ec, rs[h])
        for h in range(2, 4):
            r = rms_stats(fT, [2 * h, 2 * h + 1], T, 256, C_ZERO)
            for ec in range(2):
                gate_chunk(2 * h + ec, r)

        P.phase = kind + ":sbproj"
        P.barrier(roleA, roleB)
        proj_fm("qb", T, lambda c: (QT[:, c, 0:T], QT), scale=0.125)
        proj_fm("kb", T, lambda c: (KTst[:, c, 0:T], KTst))
        kout = ks if sample else kp
        vout = vs if sample else vp
        kst = {}

        def k_dst(b, s):
            if s == 0:
                kst[b] = tok.next()
            return kst[b][:, s * 256:(s + 1) * 256], kst[b]

        ns_ = wshapes["kb"][0]
        for s in range(ns_):
            slot = load_slab("kb", s)
            for b in range(NB):
                ps = psr.next()
                for kc in range(8):
                    P.mm(ps[:, 0:256], nT[:, kc, b * 128:(b + 1) * 128], slot[:, kc * 256:(kc + 1) * 256],
                         kc == 0, kc == 7, reads=[nT, slot], writes=[ps])
                ap, tl = k_dst(b, s)
                P.copy(ap, ps[:, 0:256], reads=[ps], writes=[tl], eng=evac_eng())
        for b in range(NB):
            P.dma(kout[ro + t0 + b * 128:ro + t0 + (b + 1) * 128, :], kst[b][:, :], reads=[kst[b]], sem=kst[b])
        vres = Res("vrows")
        if sample:
            proj_tm("vb", T, lambda b, s: (vtok_s[:, s * 256:(s + 1) * 256], vtok_s))
            P.dma(vout[ro:ro + 128, :], vtok_s[:, :].bitcast(F32), reads=[vtok_s], sem=vtok_s)
        else:
            vst = {}

            def v_dst(b, s):
                if s == 0:
                    vst[b] = tok.next()
                return vst[b][:, s * 256:(s + 1) * 256], vst[b]

            for s in range(ns_):
                slot = load_slab("vb", s)
                for b in range(NB):
                    ps = psr.next()
                    for kc in range(8):
                        P.mm(ps[:, 0:256], nT[:, kc, b * 128:(b + 1) * 128], slot[:, kc * 256:(kc + 1) * 256],
                             kc == 0, kc == 7, reads=[nT, slot], writes=[ps])
                    ap, tl = v_dst(b, s)
                    P.copy(ap, ps[:, 0:256], reads=[ps], writes=[tl], eng=evac_eng())
            for b in range(NB):
                P.dma(vout[ro + t0 + b * 128:ro + t0 + (b + 1) * 128, :], vst[b][:, :], reads=[vst[b]], writes=[vres_all],
                      sem=vst[b])
            P.dma(kts[:, :, t0:t0 + T].rearrange("c p t -> p c t"), KTst[:, :, 0:T],
                  reads=[KTst], writes=[kres_all], sem=KTst)

        P.phase = kind + ":attn"
        if not sample:
            nkb_total = (t0 + T) // 128
            for hp in range(8):
                po = psO[hp % 2]
                first = True
                nch = (nkb_total + 3) // 4
                for ch in reversed(range(nch)):
                    kb0 = ch * 4
                    nb = min(4, nkb_total - kb0)
                    kt = ktr.next()
                    va, vb_ = vtr.next()
                    P.dma(kt[:, 0:nb * 128], kts[hp, :, kb0 * 128:(kb0 + nb) * 128],
                          reads=[kres_all], writes=[kt], sem=kt, eng="sp")
                    for e, vt in ((0, va), (1, vb_)):
                        P.dma(vt[:, 0:nb, e * 64:(e + 1) * 64],
                              vp[ro + kb0 * 128:ro + (kb0 + nb) * 128, hp * 128 + e * 64: hp * 128 + (e + 1) * 64]
                              .rearrange("(b p) d -> p b d", p=128),
                              reads=[vres_all], writes=[vt], sem=vt, eng="pool")
                    for bl in reversed(range(nb)):
                        kb = kb0 + bl
                        dg = kb * 128 - t0
                        for e, vt in ((0, va), (1, vb_)):
                            h = 2 * hp + e
                            rows = slice(e * 64, (e + 1) * 64)
                            pss = psr.next()
                            P.mm(pss[:, 0:T], kt[rows, bl * 128:(bl + 1) * 128], QT[rows, hp, 0:T], True, True,
                                 reads=[kt, QT], writes=[pss])
                            ez = wk.next()
                            P.act(ez[:, 0:T], pss[:, 0:T], AF.Exp, reads=[pss, par], writes=[ez],
                                  bias=par[:, P_SBB + h:P_SBB + h + 1])
                            Lr = wkr.next()
                            P.act(Lr[:, 0:T], ez[:, 0:T], AF.Ln, reads=[ez, cst], writes=[Lr], bias=c_(C_ONE))
                            if dg >= 0:
                                md = C_MD0 if dg == 0 else C_MD1
                                P.tt(Lr[:, 0:T], Lr[:, 0:T].bitcast(F32), cst[:, md:md + T], ALU.mult,
                                     reads=[Lr, cst], writes=[Lr], eng="pool")
                            psrr = psr.next()
                            isfirst = (kb == nkb_total - 1)
                            P.mm(psrr[:, 0:T], mgt_r, Lr[:, 0:T], True, isfirst, reads=[cstr, Lr], writes=[psrr])
                            if not isfirst:
                                P.mm(psrr[:, 0:T], ones_r, lacc[e][:, 0:T], False, True,
                                     reads=[cstr, lacc[e]], writes=[psrr])
                            t1 = wk.next()
                            P.stt(t1[:, 0:T], pss[:, 0:T], par[:, P_SBB + h:P_SBB + h + 1], Lr[:, 0:T].bitcast(F32),
                                  ALU.add, ALU.subtract, reads=[pss, par, Lr], writes=[t1])
                            t2 = wk.next()
                            P.tt(t2[:, 0:T], t1[:, 0:T], psrr[:, 0:T], ALU.subtract, reads=[t1, psrr], writes=[t2])
                            A = wkb.next()
                            P.act(A[:, 0:T], t2[:, 0:T], AF.Exp, reads=[t2], writes=[A])
                            if dg >= 0:
                                P.tt(A[:, 0:T], A[:, 0:T], cst[:, md:md + T], ALU.mult,
                                     reads=[A, cst], writes=[A], eng="pool")
                            last = (kb == 0 and e == 1)
                            P.mm(po[:, 0:T], vt[:, bl, :], A[:, 0:T], first, last, reads=[vt, A], writes=[po])
                            first = False
                            if isfirst:
                                P.copy(lacc[e][:, 0:T], Lr[:, 0:T].bitcast(F32), reads=[Lr], writes=[lacc[e]],
                                       eng="pool")
                            elif kb > 0:
                                P.tt(lacc[e][:, 0:T], lacc[e][:, 0:T].bitcast(F32), Lr[:, 0:T].bitcast(F32), ALU.add,
                                     reads=[lacc[e], Lr], writes=[lacc[e]], eng="pool")
                P.copy(obT[:, hp, 0:T], po[:, 0:T], reads=[po], writes=[obT], eng=evac_eng())
        else:
            sample_attention()

        P.phase = kind + ":merge"
        mT = QT
        for s in range(4):
            s_og = load_slab("wog", s)
            pA = [fm_group(s_og, 256, j, oaT, T) for j in range(2)]
            s_ga = load_slab("ga", s)
            t1s = []
            for j in range(2):
                pGA = fm_group(s_ga, 256, j, nT, T)
                sa = wk.next()
                P.act(sa[:, 0:T], pGA[:, 0:T], AF.Sigmoid, reads=[pGA], writes=[sa])
                t1 = mrg[j]
                P.tt(t1[:, 0:T], sa[:, 0:T], pA[j][:, 0:T], ALU.mult, reads=[sa, pA[j]], writes=[t1])
                t1s.append(t1)
            s_os = load_slab("wos", s)
            pB = [fm_group(s_os, 256, j, obT, T) for j in range(2)]
            s_gb = load_slab("gb", s)
            for j in range(2):
                c = 2 * s + j
                pGB = fm_group(s_gb, 256, j, nT, T)
                sb_ = wk.next()
                P.act(sb_[:, 0:T], pGB[:, 0:T], AF.Sigmoid, reads=[pGB], writes=[sb_])
                t2 = wk.next()
                P.tt(t2[:, 0:T], sb_[:, 0:T], pB[j][:, 0:T], ALU.mult, reads=[sb_, pB[j]], writes=[t2])
                P.tt(mT[:, c, 0:T], t1s[j][:, 0:T], t2[:, 0:T], ALU.add, reads=[t1s[j], t2], writes=[mT], eng="pool")
        for s in range(4):
            slot = load_slab("wout", s)
            for j in range(2):
                c = 2 * s + j
                ps = fm_group(slot, 256, j, mT, T)
                P.copy(fT[:, c, 0:T], ps[:, 0:T], reads=[ps], writes=[fT], eng=evac_eng())
        resid_add(P_MPOST, T, C_ZERO)
        P.phase = kind + ":ffn2"
        ffn(P_F2PRE, P_F2POST, "f2g", "f2u", "f2d", T)
        P.phase = kind + ":yout"
        yout = ys if sample else yp
        for b in range(NB):
            st = tok.next()
            for half in range(2):
                ps = psr.next()
                for q in range(4):
                    c = half * 4 + q
                    P.tr(ps[:, q * 128:(q + 1) * 128], hT[:, c, b * 128:(b + 1) * 128], ident,
                         reads=[hT, cst], writes=[ps])
                P.copy(st[:, half * 512:(half + 1) * 512], ps[:, 0:512], reads=[ps], writes=[st], eng=evac_eng())
            P.dma(yout[ro + t0 + b * 128:ro + t0 + (b + 1) * 128, :], st[:, :], reads=[st], sem=st)

    vres_all = Res("vres_all")
    kres_all = Res("kres_all")
    first_blk = {}

    AR = arena

    def sample_attention():
        T = 128
        kpage_r = Res("kpage")
        vpage_r = [Res("vpage0"), Res("vpage1")]
        ktp_r = ktp_t
        qblk_r = qblk_t
        kpage = AR[:, 0:1024].bitcast(F32)
        vpage = [AR[:, 1024:2048], AR[:, 2048:3072]]
        qblk = qblk_t
        sbb64 = par[:, P_SBB64:P_SBB64 + 64]
        LA = lacc[0]
        it = 0
        for j in range(NSEQ):
            P.ts(qblk_t[:, :, :], cst[:, 0:512].rearrange("p (c n) -> p c n", n=64), 0.0, ALU.mult, reads=[cst],
                 writes=[qblk_r], eng="pool")
            for hp in range(8):
                for e in range(2):
                    h = 2 * hp + e
                    rows = slice(e * 64, (e + 1) * 64)
                    P.copy(qblk[rows, hp, h * 4:h * 4 + 4], QT[rows, hp, 4 * j:4 * j + 4],
                           reads=[QT], writes=[qblk_r], eng=evac_eng())
            nblk = NPG + 1
            for g in reversed(range(nblk)):
                new = g == NPG
                isfirst = new
                par_i = it % 2
                it += 1
                if new:
                    ktsrc = lambda hp: KTst[:, hp, 0:128]
                    kt_res = KTst
                    vsrc = vtok_s[:, :]
                    v_res = vtok_s
                else:
                    col = j * NPG + g
                    P.op("pool", lambda e_, col=col: e_.indirect_dma_start(
                        out=AR[:, 0:1024], out_offset=None, in_=ck.bitcast(F32R),
                        in_offset=bass.IndirectOffsetOnAxis(ap=idx[:, col:col + 1], axis=0)),
                        reads=[idx], writes=[kpage_r], dma="kpage")
                    P.op("pool", lambda e_, col=col, pi=par_i: e_.indirect_dma_start(
                        out=vpage[pi], out_offset=None, in_=cv.bitcast(F32R),
                        in_offset=bass.IndirectOffsetOnAxis(ap=idx[:, col:col + 1], axis=0)),
                        reads=[idx], writes=[vpage_r[par_i]], dma="vpage%d" % par_i)
                    ktpv = ktp_t[par_i]
                    for half in range(2):
                        ps = psr.next()
                        for q in range(4):
                            hp = half * 4 + q
                            P.tr(ps[:, q * 128:(q + 1) * 128], kpage[:, hp * 128:(hp + 1) * 128], ident,
                                 reads=[kpage_r, cst], writes=[ps])
                        P.copy(ktp_t[par_i][:, half * 4:(half + 1) * 4, :],
                               ps[:, 0:512].rearrange("p (c n) -> p c n", n=128), reads=[ps],
                               writes=[ktp_r[par_i]], eng=evac_eng())
                    ktsrc = lambda hp, ktpv=ktpv: ktpv[:, hp, :]
                    kt_res = ktp_r[par_i]
                    vsrc = vpage[par_i]
                    v_res = vpage_r[par_i]
                pss = psr.next()
                for hp in range(8):
                    P.mm(pss[:, 0:64], ktsrc(hp), qblk[:, hp, :], hp == 0, hp == 7, reads=[kt_res, qblk_r],
                         writes=[pss])
                z = wk.next()
                P.tt(z[:, 0:64], pss[:, 0:64], sbb64, ALU.add, reads=[pss, par], writes=[z])
                ez = wk.next()
                P.act(ez[:, 0:64], z[:, 0:64], AF.Exp, reads=[z], writes=[ez])
                Lr = wkr.next()
                P.act(Lr[:, 0:64], ez[:, 0:64], AF.Ln, reads=[ez, cst], writes=[Lr], bias=c_(C_ONE))
                if new:
                    P.stt(Lr[:, 0:64], Lr[:, 0:64].bitcast(F32), cst[:, C_SEQ + j:C_SEQ + j + 1],
                          cst[:, C_MLT:C_MLT + 64], ALU.mult, ALU.mult, reads=[Lr, cst], writes=[Lr])
                psrr = psr.next()
                P.mm(psrr[:, 0:64], mgt_r, Lr[:, 0:64], True, isfirst, reads=[cstr, Lr], writes=[psrr])
                if not isfirst:
                    P.mm(psrr[:, 0:64], ones_r, LA[:, 0:64], False, True, reads=[cstr, LA], writes=[psrr])
                t1 = wk.next()
                P.tt(t1[:, 0:64], z[:, 0:64], Lr[:, 0:64].bitcast(F32), ALU.subtract, reads=[z, Lr], writes=[t1])
                t2 = wk.next()
                P.tt(t2[:, 0:64], t1[:, 0:64], psrr[:, 0:64], ALU.subtract, reads=[t1, psrr], writes=[t2])
                A = wkr.next()
                P.act(A[:, 0:64], t2[:, 0:64], AF.Exp, reads=[t2], writes=[A])
                if new:
                    P.stt(A[:, 0:64], A[:, 0:64].bitcast(F32), cst[:, C_SEQ + j:C_SEQ + j + 1],
                          cst[:, C_MLT:C_MLT + 64], ALU.mult, ALU.mult, reads=[A, cst], writes=[A])
                last = g == 0
                for half in range(2):
                    P.mm(psO[half][0:64, 0:512], A[:, 0:64], vsrc[:, half * 512:(half + 1) * 512], isfirst, last,
                         reads=[A, v_res], writes=[psO[half]])
                if isfirst:
                    P.copy(LA[:, 0:64], Lr[:, 0:64].bitcast(F32), reads=[Lr], writes=[LA], eng="pool")
                elif not last:
                    P.tt(LA[:, 0:64], LA[:, 0:64].bitcast(F32), Lr[:, 0:64].bitcast(F32), ALU.add,
                         reads=[LA, Lr], writes=[LA], eng="pool")
            prod = tok.next()
            for half in range(2):
                P.tt(prod[0:64, half * 512:(half + 1) * 512], psO[half][0:64, 0:512],
                     cst[0:64, C_HM + half * 512:C_HM + (half + 1) * 512], ALU.mult,
                     reads=[psO[half], cst], writes=[prod])
            osel = wk.next()
            P.op("dve", lambda e_, prod=prod, osel=osel: e_.tensor_reduce(
                osel[0:64, 0:64], prod[0:64, :].rearrange("p (h d) -> p d h", d=64), AX.X, ALU.add),
                reads=[prod], writes=[osel])
            o2 = wk.next()
            P.ts(o2[0:64, 0:64], osel[0:64, 0:64], cst[0:64, C_EVEN:C_EVEN + 1], ALU.mult, reads=[osel, cst],
                 writes=[o2])
            P.ts(o2[0:64, 64:128], osel[0:64, 0:64], cst[0:64, C_ODD:C_ODD + 1], ALU.mult, reads=[osel, cst],
                 writes=[o2])
            pst = psr.next()
            P.tr(pst[:, 0:64], o2[0:64, 0:128], cst[0:64, C_ID:C_ID + 64], reads=[o2, cst], writes=[pst])
            pv = pst[:, 0:64].rearrange("p (c e t) -> p c e t", e=2, t=4)
            P.copy(obT[0:64, :, 4 * j:4 * j + 4], pv[0:64, :, 0, :], reads=[pst], writes=[obT], eng=evac_eng())
            P.copy(obT[64:128, :, 4 * j:4 * j + 4], pv[64:128, :, 1, :], reads=[pst], writes=[obT], eng=evac_eng())
        P.ts(obT[:, :, 64:128], cst[:, 0:512].rearrange("p (b d) -> p b d", d=64), 0.0, ALU.mult, reads=[cst],
             writes=[obT], eng="pool")

    for qi in range(NPR):
        for i, (t0, T) in enumerate(tiles):
            do_tile("prompt", t0, T, i == len(tiles) - 1, qi)
    for qi in range(NSG):
        do_tile("sample", 0, 128, False, qi)
    P.emit()
    return nc


_cache = {}
NCORES = 8


def kernel(x_prompt, x_sample, cache_k, cache_v, state_gla, page_table, meta_tokens,
           ffn1_pre_g, ffn1_w_gate, ffn1_w_up, ffn1_w_down, ffn1_post_g,
           mix_pre_g, w_in, w_gk2, b_gk, gla_norm_g, sb_bias, w_o_gla, w_o_sb, w_out, mix_post_g,
           ffn2_pre_g, ffn2_w_gate, ffn2_w_up, ffn2_w_down, ffn2_post_g):
    f32 = np.float32
    x_prompt = np.asarray(x_prompt, f32)
    x_sample = np.asarray(x_sample, f32)
    B, SEQ, _ = x_prompt.shape
    DB, DS, _ = x_sample.shape
    NPOOL = cache_k.shape[1]
    NPG = page_table.shape[1]
    L = N_META + SEQ
    n256 = L // 256
    rem = L - n256 * 256
    LPAD = n256 * 256 + (128 if rem else 0)
    ncores = NCORES
    while (DB // NSEQ) % ncores or (ncores < B and B % ncores):
        ncores //= 2
    NPR = max(1, B // ncores)
    NSG = DB // NSEQ // ncores
    assert DS == 4 and DB % NSEQ == 0

    win = np.asarray(w_in[0], f32)
    offs = np.cumsum([0, 512, 512, 1024, 16, 1024, 1024, 1024, 1024, 1024, 1024])
    seg = {n: win[:, offs[i]:offs[i + 1]] for i, n in
           enumerate(["qa", "ka", "va", "gl", "ra", "qb", "kb", "vb", "ga", "gb"])}
    glpad = np.zeros((D, 128), f32)
    glpad[:, :16] = seg["gl"]
    W = {
        "f1g": slabs(np.asarray(ffn1_w_gate[0], f32), 256), "f1u": slabs(np.asarray(ffn1_w_up[0], f32), 256),
        "f1d": slabs(np.asarray(ffn1_w_down[0], f32), 128),
        "qa": slabs(seg["qa"], 256), "ka": slabs(seg["ka"], 256), "va": slabs(seg["va"], 256),
        "gl": slabs(glpad, 128), "ra": slabs(seg["ra"], 256), "qb": slabs(seg["qb"], 256),
        "kb": slabs(seg["kb"], 256), "vb": slabs(seg["vb"], 256), "ga": slabs(seg["ga"], 256),
        "gb": slabs(seg["gb"], 256),
        "wog": slabs(np.asarray(w_o_gla[0], f32), 256), "wos": slabs(np.asarray(w_o_sb[0], f32), 256),
        "wout": slabs(np.asarray(w_out[0], f32), 256),
        "f2g": slabs(np.asarray(ffn2_w_gate[0], f32), 256), "f2u": slabs(np.asarray(ffn2_w_up[0], f32), 256),
        "f2d": slabs(np.asarray(ffn2_w_down[0], f32), 128),
    }
    wshapes = {n: tuple(W[n].shape) for n in WNAMES}
    cfg = dict(L=L, LPAD=LPAD, NPG=NPG, NPOOL=NPOOL, NPROMPT=NPR, NSG=NSG, wshapes=wshapes)
    key = (L, NPG, NPOOL, NPR, NSG)
    if key not in _cache:
        _cache[key] = build(cfg)
    nc = _cache[key]

    cst, cstr = make_consts(rem if rem else 128)
    par = np.zeros((128, P_TOT), f32)

    def gcol(g):
        return np.asarray(g[0], f32).reshape(8, 128).T

    par[:, P_F1PRE:P_F1PRE + 8] = gcol(ffn1_pre_g)
    par[:, P_F1POST:P_F1POST + 8] = gcol(ffn1_post_g)
    par[:, P_MPRE:P_MPRE + 8] = gcol(mix_pre_g)
    par[:, P_MPOST:P_MPOST + 8] = gcol(mix_post_g)
    par[:, P_F2PRE:P_F2PRE + 8] = gcol(ffn2_pre_g)
    par[:, P_F2POST:P_F2POST + 8] = gcol(ffn2_post_g)
    par[:, P_GLAG:P_GLAG + 2] = np.asarray(gla_norm_g[0], f32).reshape(2, 128).T
    sbv = np.asarray(sb_bias[0], f32)
    par[:, P_SBB:P_SBB + 16] = np.broadcast_to(sbv[None, :], (128, 16))
    par[:, P_SBB64:P_SBB64 + 64] = np.broadcast_to(np.repeat(sbv, 4)[None, :], (128, 64))

    ckf = np.asarray(cache_k[0], f32).reshape(NPOOL * 128, D)
    cvf = np.asarray(cache_v[0], f32).reshape(NPOOL * 128, D)
    meta = np.asarray(meta_tokens, f32)
    pt = np.asarray(page_table, np.int32)
    sg = np.asarray(state_gla[0], f32)
    in_maps = []
    for c in range(ncores):
        xpc = np.zeros((NPR * LPAD, D), f32)
        for i in range(NPR):
            xpc[i * LPAD:i * LPAD + N_META] = meta
            xpc[i * LPAD + N_META:i * LPAD + L] = x_prompt[(c * NPR + i) % B]
        xsc = np.zeros((NSG, 128, D), f32)
        s0 = c * NSG * NSEQ
        xsc[:, :NSEQ * 4] = x_sample[s0:s0 + NSG * NSEQ].reshape(NSG, NSEQ * 4, D)
        m = dict(xp=xpc, xs=xsc.reshape(NSG * 128, D), ck=ckf, cv=cvf,
                 ptb=np.ascontiguousarray(pt[s0:s0 + NSG * NSEQ].reshape(NSG, NSEQ * NPG)),
                 sg=np.ascontiguousarray(sg[s0:s0 + NSG * NSEQ].transpose(0, 2, 1, 3)).reshape(NSG * NSEQ, 128, 1024),
                 cst=cst, cstr=cstr, par=par,
                 wgk=np.asarray(w_gk2[0], f32), bgk=np.asarray(b_gk, f32).reshape(1, 512))
        for n in WNAMES:
            m["w_" + n] = W[n]
        in_maps.append(m)
    res = run_bass_kernel_spmd(nc, in_maps, core_ids=list(range(ncores)))
    R = res.results

    npc = B // NPR

    def prow(name):
        return np.concatenate([R[c][name].reshape(NPR, LPAD, D) for c in range(npc)])

    def srow(name):
        return np.concatenate([R[c][name].reshape(NSG, 128, D)[:, :NSEQ * 4].reshape(NSG * NSEQ, 4, D)
                               for c in range(ncores)])

    y_prompt = prow("yp")[:, N_META:L]
    k_prompt = prow("kp")[:, :L].reshape(B, L, 16, 64)[None]
    v_prompt = prow("vp")[:, :L].reshape(B, L, 16, 64)[None]
    gsp = np.concatenate([R[c]["spo"].reshape(NPR, 128, 4, 256) for c in range(npc)]).transpose(0, 2, 1, 3)[None]
    y_sample = srow("ys")
    k_sample = srow("ks").reshape(DB, 4, 16, 64)[None]
    v_sample = srow("vs").reshape(DB, 4, 16, 64)[None]
    gss = np.concatenate([R[c]["sso"].reshape(NSG * NSEQ, 128, 4, 256) for c in range(ncores)]).transpose(0, 2, 1, 3)[None]
    return (np.ascontiguousarray(y_prompt, f32), np.ascontiguousarray(y_sample, f32),
            np.ascontiguousarray(k_prompt, f32), np.ascontiguousarray(v_prompt, f32),
            np.ascontiguousarray(gsp, f32), np.ascontiguousarray(k_sample, f32),
            np.ascontiguousarray(v_sample, f32), np.ascontiguousarray(gss, f32))
```

```python
from contextlib import ExitStack
import math
import numpy as np
import concourse.bass as bass
import concourse.mybir as mybir
from concourse.bass_utils import run_bass_kernel_spmd

F32 = mybir.dt.float32
F32R = mybir.dt.float32r
BF16 = mybir.dt.bfloat16
I32 = mybir.dt.int32
AF = mybir.ActivationFunctionType
ALU = mybir.AluOpType
AX = mybir.AxisListType

ENGS = ("pe", "act", "dve", "pool", "sp")
SAME_ENGINE_SYNC = True

D = 1024
DFF = 2816
FC = 22
N_META = 16
EPS = 1e-6
NSEQ = 16
WSLOT = 2816
NW = 4


class Res:
    __slots__ = ("name", "w", "r")

    def __init__(self, name):
        self.name = name
        self.w = {}
        self.r = {}


class TT:
    def __init__(self, h, name):
        self.h = h
        self.res = Res(name)
        self.name = name

    def __getitem__(self, k):
        return self.h[k]


class FV:
    def __init__(self, t):
        self.h = t.h.bitcast(F32) if hasattr(t.h, "bitcast") else t.h[:].bitcast(F32)
        self.res = t.res
        self.name = t.name

    def __getitem__(self, k):
        return self.h[k]


class Ring:
    def __init__(self, tiles):
        self.t = tiles
        self.i = 0

    def next(self):
        t = self.t[self.i % len(self.t)]
        self.i += 1
        return t


class Prog:
    def __init__(self, nc):
        self.nc = nc
        self.es = ExitStack()
        self.streams = {e: [] for e in ENGS}
        self.count = {e: 0 for e in ENGS}
        self.waited = {e: {} for e in ENGS}
        self.dmacount = {}
        self.phase = ""
        self.tags = {e: [] for e in ENGS}

    def sb(self, name, shape, dt=F32):
        h = self.es.enter_context(self.nc.sbuf_tensor("t_" + name, list(shape), dt))
        return TT(h, name)

    def ps(self, name, shape=(128, 512), dt=F32):
        h = self.es.enter_context(self.nc.psum_tensor("t_" + name, list(shape), dt))
        return TT(h, name)

    def op(self, eng, fn, reads=(), writes=(), dma=None):
        reads = [x.res if isinstance(x, (TT, FV)) else x for x in reads]
        writes = [x.res if isinstance(x, (TT, FV)) else x for x in writes]
        deps = {}

        def add(d):
            for k, v in d.items():
                if deps.get(k, 0) < v:
                    deps[k] = v

        for r in reads:
            add(r.w)
        for w in writes:
            add(w.w)
            add(w.r)
        if dma is None:
            self.count[eng] += 1
            ev = (eng, self.count[eng])
        else:
            key = dma.res.name if isinstance(dma, (TT, FV)) else (dma.name if isinstance(dma, Res) else dma)
            key = "d_" + key
            self.dmacount[key] = self.dmacount.get(key, 0) + 16
            ev = (key, self.dmacount[key])
        waits = []
        for k, v in deps.items():
            if k == eng and (eng == "pe" or not SAME_ENGINE_SYNC):
                continue
            if k.startswith("d_") and k != ev[0]:
                v = max(v, self.dmacount.get(k, 0))
            if self.waited[eng].get(k, 0) >= v:
                continue
            self.waited[eng][k] = v
            waits.append((k, v))
        self.streams[eng].append((waits, fn, ev))
        self.tags[eng].append(self.phase)
        for r in reads:
            if r.r.get(ev[0], 0) < ev[1]:
                r.r[ev[0]] = ev[1]
        for w in writes:
            w.w[ev[0]] = ev[1]
            w.r = {}
        return ev

    def barrier(self, froms, tos):
        for t in tos:
            t = t.res if isinstance(t, (TT, FV)) else t
            for f in froms:
                f = f.res if isinstance(f, (TT, FV)) else f
                for d in (f.w, f.r):
                    for k, v in d.items():
                        if t.r.get(k, 0) < v:
                            t.r[k] = v

    def emit(self):
        nc = self.nc
        es = self.es
        keys = list(ENGS) + sorted(self.dmacount.keys())
        sems = {k: es.enter_context(nc.semaphore("s_" + k)) for k in keys}
        fin = []
        for e in ENGS:
            if e != "sp" and self.count[e] > 0:
                fin.append((e, self.count[e]))
        for k, v in self.dmacount.items():
            fin.append((k, v))
        streams = self.streams
        block = es.enter_context(nc.Block())

        def make(name):
            def body(eng):
                for waits, fn, ev in streams[name]:
                    for k, v in waits:
                        eng.wait_ge(sems[k], v)
                    inst = fn(eng)
                    inst.then_inc(sems[ev[0]], 16 if ev[0].startswith("d_") else 1)
                if name == "sp":
                    for k, v in fin:
                        eng.wait_ge(sems[k], v)
            return body

        block.sync(make("sp"))
        block.scalar(make("act"))
        block.vector(make("dve"))
        block.gpsimd(make("pool"))
        block.tensor(make("pe"))
        es.close()
        return nc

    def dma(self, out, in_, reads=(), writes=(), sem=None, eng="sp"):
        return self.op(eng, lambda e: e.dma_start(out=out, in_=in_), reads=reads, writes=writes, dma=sem)

    def mm(self, out, lhsT, rhs, start, stop, reads, writes):
        return self.op("pe", lambda e: e.matmul(out, lhsT, rhs, start=start, stop=stop), reads=reads, writes=writes)

    def tr(self, out, in_, ident, reads, writes):
        return self.op("pe", lambda e: e.transpose(out, in_, ident), reads=reads, writes=writes)

    def act(self, out, in_, func, reads, writes, bias=None, scale=None):
        kw = {}
        if bias is not None:
            kw["bias"] = bias
        if scale is not None:
            kw["scale"] = scale
        return self.op("act", lambda e: e.activation(out, in_, func, **kw), reads=reads, writes=writes)

    def tt(self, out, in0, in1, op, reads, writes, eng="dve"):
        return self.op(eng, lambda e: e.tensor_tensor(out, in0, in1, op), reads=reads, writes=writes)

    def ts(self, out, in0, s1, op0, reads, writes, s2=None, op1=None, eng="dve"):
        if op1 is None:
            return self.op(eng, lambda e: e.tensor_scalar(out, in0, s1, None, op0), reads=reads, writes=writes)
        return self.op(eng, lambda e: e.tensor_scalar(out, in0, s1, s2, op0, op1), reads=reads, writes=writes)

    def stt(self, out, in0, scalar, in1, op0, op1, reads, writes):
        return self.op("dve", lambda e: e.scalar_tensor_tensor(out, in0, scalar, in1, op0, op1),
                       reads=reads, writes=writes)

    def copy(self, out, in_, reads, writes, eng="dve"):
        if eng == "act":
            return self.op("act", lambda e: e.copy(out, in_), reads=reads, writes=writes)
        return self.op(eng, lambda e: e.tensor_copy(out, in_), reads=reads, writes=writes)

    def memset(self, ap, val, writes, eng="pool"):
        return self.op(eng, lambda e: e.memset(ap, val), writes=writes)


C_ID, C_LE, C_GT, C_LES, C_GTS = 0, 128, 256, 384, 512
C_MD0, C_MD1 = 640, 896
C_MLT = 1152
C_SEQ = 1216
C_VP, C_VS, C_PCOL = 1232, 1233, 1234
C_ONES = 1235
C_EPS, C_LNH, C_ZERO, C_ONE = 1363, 1364, 1365, 1366
C_EVEN, C_ODD = 1367, 1368
C_HM = 1369
C_TOT = C_HM + 1024
R_GT, R_ONES, R_NGE, R_NONES = 0, 128, 256, 384
R_TOT = 512
P_F1PRE, P_F1POST, P_MPRE, P_MPOST, P_F2PRE, P_F2POST = 0, 8, 16, 24, 32, 40
P_GLAG, P_SBB, P_SBB64 = 48, 50, 66
P_TOT = 130


def make_consts(rem):
    c = np.zeros((128, C_TOT), np.float32)
    p = np.arange(128)
    c[:, C_ID:C_ID + 128] = np.eye(128)
    c[:, C_LE:C_LE + 128] = (p[:, None] <= p[None, :])
    c[:, C_GT:C_GT + 128] = (p[:, None] > p[None, :])
    same = (p[:, None] // 4 == p[None, :] // 4) & (p[:, None] < 64) & (p[None, :] < 64)
    c[:, C_LES:C_LES + 128] = same & (p[:, None] <= p[None, :])
    c[:, C_GTS:C_GTS + 128] = same & (p[:, None] > p[None, :])
    col = np.arange(256)
    c[:, C_MD0:C_MD0 + 256] = (p[:, None] < col[None, :])
    c[:, C_MD1:C_MD1 + 256] = (p[:, None] + 128 < col[None, :])
    ht = np.arange(64)
    c[:, C_MLT:C_MLT + 64] = ((p[:, None] % 4) < (ht[None, :] % 4))
    c[:, C_SEQ:C_SEQ + 16] = (p[:, None] // 4 == np.arange(16)[None, :]) & (p[:, None] < 64)
    c[:, C_VP] = p < rem
    c[:, C_VS] = p < 64
    c[:, C_PCOL] = p
    c[:, C_ONES:C_ONES + 128] = 1.0
    c[:, C_EPS] = EPS
    c[:, C_LNH] = math.log(0.5)
    c[:, C_ZERO] = 0.0
    c[:, C_ONE] = 1.0
    hh = p // 4
    c[:, C_EVEN] = (hh % 2 == 0) & (p < 64)
    c[:, C_ODD] = (hh % 2 == 1) & (p < 64)
    hm = np.zeros((128, 16, 64), np.float32)
    for q in range(64):
        hm[q, q // 4, :] = 1.0
    c[:, C_HM:C_HM + 1024] = hm.reshape(128, 1024)
    r = np.zeros((128, R_TOT), np.float32)
    r[:, R_GT:R_GT + 128] = c[:, C_GT:C_GT + 128]
    r[:, R_ONES:R_ONES + 128] = 1.0
    r[:, R_NGE:R_NGE + 128] = -(p[:, None] >= p[None, :]).astype(np.float32)
    r[:, R_NONES:R_NONES + 128] = -1.0
    return c, r


def slabs(W, ncol):
    K, N = W.shape
    kc = K // 128
    a = W.reshape(kc, 128, N // ncol, ncol).transpose(2, 1, 0, 3)
    return np.ascontiguousarray(a).reshape(N // ncol, 128, kc * ncol)


WNAMES = ["f1g", "f1u", "f1d", "qa", "ka", "va", "gl", "ra", "qb", "kb", "vb", "ga", "gb",
          "wog", "wos", "wout", "f2g", "f2u", "f2d"]


def build(cfg):
    L = cfg["L"]
    NPG = cfg["NPG"]
    NPOOL = cfg["NPOOL"]
    NPR = cfg["NPROMPT"]
    NSG = cfg["NSG"]
    wshapes = cfg["wshapes"]
    n256 = L // 256
    rem = L - 256 * n256
    tiles = [(i * 256, 256) for i in range(n256)]
    if rem:
        assert rem <= 128
        tiles.append((n256 * 256, 128))
    LPAD = tiles[-1][0] + tiles[-1][1]
    assert LPAD == cfg["LPAD"]

    nc = bass.Bass("TRN2", target_bir_lowering=False)

    def din(name, shape, dt=F32):
        return nc.dram_tensor(name, list(shape), dt, kind="ExternalInput").ap()

    def dout(name, shape, dt=F32):
        return nc.dram_tensor(name, list(shape), dt, kind="ExternalOutput").ap()

    xp = din("xp", [NPR * LPAD, D])
    xs = din("xs", [NSG * 128, D])
    ck = din("ck", [NPOOL * 128, D])
    cv = din("cv", [NPOOL * 128, D])
    ptb = din("ptb", [NSG, NSEQ * NPG], I32)
    sg_in = din("sg", [NSG * NSEQ, 128, 1024])
    cst_d = din("cst", [128, C_TOT])
    cstr_d = din("cstr", [128, R_TOT])
    par_d = din("par", [128, P_TOT])
    wgk_d = din("wgk", [16, 512])
    bgk_d = din("bgk", [1, 512])
    WS = {n: din("w_" + n, wshapes[n]) for n in WNAMES}

    yp = dout("yp", [NPR * LPAD, D])
    ys = dout("ys", [NSG * 128, D])
    kp = dout("kp", [NPR * LPAD, D])
    vp = dout("vp", [NPR * LPAD, D])
    sp_out = dout("spo", [NPR * 128, 1024])
    ks = dout("ks", [NSG * 128, D])
    vs = dout("vs", [NSG * 128, D])
    ss_out = dout("sso", [NSG * NSEQ, 128, 1024])
    kts = nc.dram_tensor("kts", [8, 128, LPAD], BF16).ap()

    wsc = {n: nc.dram_tensor("wsc_" + n, list(wshapes[n]), BF16).ap() for n in WNAMES}
    P = Prog(nc)
    wres = {n: Res("wsc_" + n) for n in WNAMES}
    cst = P.sb("cst", [128, C_TOT])
    cstr = P.sb("cstr", [128, R_TOT], F32R)
    par = P.sb("par", [128, P_TOT])
    wgk = P.sb("wgk", [16, 512])
    bgk = P.sb("bgk", [1, 512])
    idx = P.sb("idx", [128, NSEQ * NPG], I32)
    ptt = P.sb("ptt", [128, NSEQ * NPG], I32)
    wring = Ring([P.sb("wr%d" % i, [128, WSLOT], BF16) for i in range(NW)])
    tok = Ring([P.sb("tok%d" % i, [128, D]) for i in range(3)])
    hT = P.sb("hT", [128, 8, 256])
    nT = P.sb("nT", [128, 8, 256], BF16)
    fT = P.sb("fT", [128, 8, 256])
    arena = P.sb("arena", [128, 3072], F32R)
    actT = P.sb("actT", [128, FC, 256], BF16)
    QT = P.sb("QT", [128, 8, 256], BF16)
    KTst = P.sb("KTst", [128, 8, 256], BF16)
    ktr = Ring([P.sb("kt%d" % i, [128, 512], BF16) for i in range(2)])
    vtr = Ring([(P.sb("vta%d" % i, [128, 4, 128], BF16), P.sb("vtb%d" % i, [128, 4, 128], BF16)) for i in range(2)])
    ktp_t = [P.sb("ktp%d" % i, [128, 8, 128], BF16) for i in range(2)]
    qblk_t = P.sb("qblk", [128, 8, 64], BF16)
    wkb = Ring([P.sb("wkb%d" % i, [128, 256], BF16) for i in range(3)])
    sq = Ring([P.sb("sq%d" % i, [128, 256], F32R) for i in range(2)])
    rstd = Ring([P.sb("rstd%d" % i, [128, 256]) for i in range(2)])
    wk = Ring([P.sb("wk%d" % i, [128, 256]) for i in range(6)])
    wkr = Ring([P.sb("wkr%d" % i, [128, 256], F32R) for i in range(4)])
    big = Ring([P.sb("big%d" % i, [128, 512]) for i in range(1)])
    mrg = [P.sb("mrg%d" % i, [128, 256]) for i in range(2)]
    arena2 = P.sb("arena2", [128, 8704], F32R)

    def view(a, n, name, pat=None, **kw):
        ap = arena2[:, a:a + n]
        if pat:
            ap = ap.rearrange(pat, **kw)
        return TT(ap, name)

    qaT = view(0, 1024, "qaT", "p (c t) -> p c t", t=256)
    kaT = view(1024, 1024, "kaT", "p (c t) -> p c t", t=256)
    ka_tok = view(2048, 1024, "ka_tok", "p (b t) -> p b t", t=512)
    va_tok = view(3072, 2048, "va_tok", "p (b t) -> p b t", t=1024)
    sp_tok = view(5120, 1024, "sp_tok", "p (b t) -> p b t", t=512)
    gq = view(6144, 512, "gq", "p (h t) -> p h t", t=128)
    gk_ = view(6656, 512, "gk_", "p (h t) -> p h t", t=128)
    ge = view(7168, 512, "ge", "p (h t) -> p h t", t=128)
    ge1 = view(7680, 512, "ge1", "p (h t) -> p h t", t=128)
    roleA = [qaT, kaT, ka_tok, va_tok, sp_tok, gq, gk_, ge, ge1]
    qaTf, kaTf, ka_tokf, va_tokf, sp_tokf, gqf, gk_f, gef, ge1f = [FV(t) for t in roleA]
    lacc = [view(7168 + 256 * i, 256, "lacc%d" % i) for i in range(2)]
    vtok_s = view(7680, 1024, "vtok_s")
    roleB = lacc + [vtok_s]
    glT = P.sb("glT", [128, 256])
    oaT = P.sb("oaT", [128, 8, 256], BF16)
    obT = P.sb("obT", [128, 8, 256], BF16)
    S = P.sb("S", [128, 1024])
    Ssm = S
    ost = P.sb("ost", [128, 8, 64])
    psr = Ring([P.ps("ps%d" % i) for i in range(6)])
    psO = [P.ps("psO%d" % i) for i in range(2)]

    def c_(a, n=1):
        return cst[:, a:a + n]

    ident = cst[:, C_ID:C_ID + 128]
    ones_r = cstr[:, R_ONES:R_ONES + 128]
    mgt_r = cstr[:, R_GT:R_GT + 128]
    nge_r = cstr[:, R_NGE:R_NGE + 128]
    nones_r = cstr[:, R_NONES:R_NONES + 128]

    P.dma(cst[:], cst_d, writes=[cst], sem=cst)
    P.dma(cstr[:], cstr_d.bitcast(F32R), writes=[cstr], sem=cstr, eng="pool")
    P.dma(par[:], par_d, writes=[par], sem=par)
    P.dma(wgk[:], wgk_d, writes=[wgk], sem=wgk)
    P.dma(bgk[:], bgk_d, writes=[bgk], sem=bgk)
    for (va_, vb2) in vtr.t:
        zsrc = cst[:, 0:256].rearrange("p (b d) -> p b d", d=64)
        P.ts(va_[:, :, 64:128], zsrc, 0.0, ALU.mult, reads=[cst], writes=[va_], eng="pool")
        P.ts(vb2[:, :, 0:64], zsrc, 0.0, ALU.mult, reads=[cst], writes=[vb2], eng="pool")

    def load_slab(name, s):
        slot = wring.next()
        F = wshapes[name][2]
        P.dma(slot[:, 0:F], wsc[name][s], reads=[wres[name]], writes=[slot], sem=slot, eng="sp")
        return slot

    def convert_weights():
        for name in WNAMES:
            F = wshapes[name][2]
            nsp = 1 if F <= 2048 else 2
            w_ = F // nsp
            for s in range(wshapes[name][0]):
                slot = wring.next()
                for q in range(nsp):
                    P.dma(slot[:, q * w_:(q + 1) * w_], WS[name][s][:, q * w_:(q + 1) * w_], writes=[slot],
                          sem=slot, eng="pool")
                P.dma(wsc[name][s], slot[:, 0:F], reads=[slot], writes=[wres[name]], sem=slot, eng="pool")

    ev_i = [0]

    def evac_eng():
        ev_i[0] += 1
        return "act" if ev_i[0] % 2 else "dve"

    def rms_stats(src, chunks, T, dn, lnb):
        ps = psr.next()
        n = len(chunks)
        for i, c in enumerate(chunks):
            s = sq.next()
            P.tt(s[:, 0:T], src[:, c, 0:T], src[:, c, 0:T], ALU.mult, reads=[src], writes=[s], eng="pool")
            P.mm(ps[:, 0:T], ones_r, s[:, 0:T], i == 0, i == n - 1, reads=[cstr, s], writes=[ps])
        lv = wk.next()
        P.act(lv[:, 0:T], ps[:, 0:T], AF.Ln, reads=[ps, cst], writes=[lv], scale=1.0 / dn, bias=c_(C_EPS))
        r = rstd.next()
        P.act(r[:, 0:T], lv[:, 0:T], AF.Exp, reads=[lv, cst], writes=[r], scale=-0.5, bias=c_(lnb))
        return r

    def norm_to_nT(goff, T):
        r = rms_stats(hT, list(range(8)), T, D, C_ZERO)
        for c in range(8):
            P.stt(nT[:, c, 0:T], hT[:, c, 0:T], par[:, goff + c:goff + c + 1], r[:, 0:T], ALU.mult, ALU.mult,
                  reads=[hT, par, r], writes=[nT])

    def resid_add(goff, T, lnb):
        r = rms_stats(fT, list(range(8)), T, D, lnb)
        for c in range(8):
            t = wk.next()
            P.stt(t[:, 0:T], fT[:, c, 0:T], par[:, goff + c:goff + c + 1], r[:, 0:T], ALU.mult, ALU.mult,
                  reads=[fT, par, r], writes=[t])
            P.tt(hT[:, c, 0:T], hT[:, c, 0:T], t[:, 0:T], ALU.add, reads=[hT, t], writes=[hT], eng="pool")

    def fm_group(slot, ncol, j, rhsT, T, KC=8):
        ps = psr.next()
        for kc in range(KC):
            P.mm(ps[:, 0:T], slot[:, kc * ncol + j * 128: kc * ncol + (j + 1) * 128], rhsT[:, kc, 0:T],
                 kc == 0, kc == KC - 1, reads=[slot, rhsT], writes=[ps])
        return ps

    def ffn(pre, post, wg, wu, wd, T):
        norm_to_nT(pre, T)
        aT = actT
        for s in range(FC // 2):
            sg = load_slab(wg, s)
            su = load_slab(wu, s)
            for j in range(2):
                f = 2 * s + j
                pg = fm_group(sg, 256, j, nT, T)
                pu = fm_group(su, 256, j, nT, T)
                sl = wk.next()
                P.act(sl[:, 0:T], pg[:, 0:T], AF.Silu, reads=[pg], writes=[sl])
                P.tt(aT[:, f, 0:T], sl[:, 0:T], pu[:, 0:T], ALU.mult, reads=[sl, pu], writes=[actT])
        for c in range(8):
            sd = load_slab(wd, c)
            pd = psr.next()
            for f in range(FC):
                P.mm(pd[:, 0:T], sd[:, f * 128:(f + 1) * 128], aT[:, f, 0:T], f == 0, f == FC - 1,
                     reads=[sd, actT], writes=[pd])
            P.copy(fT[:, c, 0:T], pd[:, 0:T], reads=[pd], writes=[fT], eng=evac_eng())
        resid_add(post, T, C_LNH)

    def proj_fm(name, T, dst, scale=None, nchunks=None):
        ns = wshapes[name][0]
        for s in range(ns):
            slot = load_slab(name, s)
            for j in range(2):
                c = 2 * s + j
                ps = fm_group(slot, 256, j, nT, T)
                ap, tl = dst(c)
                if scale is None:
                    P.copy(ap, ps[:, 0:T], reads=[ps], writes=[tl], eng=evac_eng())
                else:
                    P.op("act", lambda e, ap=ap, ps=ps: e.mul(ap, ps[:, 0:T], scale), reads=[ps], writes=[tl])

    def proj_tm(name, T, dst):
        ns = wshapes[name][0]
        for s in range(ns):
            slot = load_slab(name, s)
            for b in range(T // 128):
                ps = psr.next()
                for kc in range(8):
                    P.mm(ps[:, 0:256], nT[:, kc, b * 128:(b + 1) * 128], slot[:, kc * 256:(kc + 1) * 256],
                         kc == 0, kc == 7, reads=[nT, slot], writes=[ps])
                ap, tl = dst(b, s)
                P.copy(ap, ps[:, 0:256], reads=[ps], writes=[tl], eng=evac_eng())

    def do_tile(kind, t0, T, last_prompt, qi=0):
        NB = T // 128
        sample = kind == "sample"
        xsrc = xs if sample else xp
        ro = qi * (128 if sample else LPAD)
        so = qi * NSEQ
        if sample:
            P.dma(ptt[:], ptb[qi:qi + 1, :].partition_broadcast(128), writes=[ptt], sem=ptt)
            P.ts(idx[:], ptt[:], 128.0, ALU.mult, reads=[ptt, cst], writes=[idx], s2=c_(C_PCOL), op1=ALU.add)
        elif t0 == 0:
            P.memset(S[:], 0.0, [S])
        P.phase = kind + ":load"
        sts = []
        for b in range(NB):
            st = tok.next()
            P.dma(st[:, :], xsrc[ro + t0 + b * 128: ro + t0 + (b + 1) * 128, :], writes=[st], sem=st)
            sts.append(st)
        for c in range(8):
            ps = psr.next()
            for b in range(NB):
                P.tr(ps[:, b * 128:(b + 1) * 128], sts[b][:, c * 128:(c + 1) * 128], ident,
                     reads=[sts[b], cst], writes=[ps])
            P.copy(hT[:, c, 0:T], ps[:, 0:T], reads=[ps], writes=[hT], eng=evac_eng())
        P.phase = kind + ":ffn1"
        ffn(P_F1PRE, P_F1POST, "f1g", "f1u", "f1d", T)
        P.phase = kind + ":mixnorm"
        norm_to_nT(P_MPRE, T)
        P.phase = kind + ":gla"
        P.barrier(roleB, roleA)
        proj_fm("qa", T, lambda c: (qaT[:, c, 0:T], qaT), scale=128 ** -0.5)
        proj_fm("ka", T, lambda c: (kaT[:, c, 0:T], kaT))
        slot = load_slab("gl", 0)
        ps = fm_group(slot, 128, 0, nT, T)
        P.copy(glT[:, 0:T], ps[:, 0:T], reads=[ps], writes=[glT], eng=evac_eng())
        proj_tm("ka", T, lambda b, s: (ka_tok[:, b, s * 256:(s + 1) * 256], ka_tok))
        proj_tm("va", T, lambda b, s: (va_tok[:, b, s * 256:(s + 1) * 256], va_tok))
        for b in range(NB):
            ps = psr.next()
            P.mm(ps[:, 0:512], glT[0:16, b * 128:(b + 1) * 128], wgk[0:16, :], True, False,
                 reads=[glT, wgk], writes=[ps])
            P.mm(ps[:, 0:512], cst[0:1, C_ONES:C_ONES + 128], bgk[0:1, :], False, True,
                 reads=[cst, bgk], writes=[ps])
            e1 = big.next()
            P.act(e1[:, :], ps[:, 0:512], AF.Exp, reads=[ps], writes=[e1], scale=-1.0)
            if sample or (last_prompt and b == NB - 1):
                e2_ = tok.next()
                P.act(e2_[:, 0:512], e1[:, :], AF.Ln, reads=[e1, cst], writes=[e2_], bias=c_(C_ONE))
                vm = c_(C_VS) if sample else c_(C_VP)
                P.ts(sp_tok[:, b, :], e2_[:, 0:512], vm, ALU.mult, reads=[e2_, cst], writes=[sp_tok])
            else:
                P.act(sp_tok[:, b, :], e1[:, :], AF.Ln, reads=[e1, cst], writes=[sp_tok], bias=c_(C_ONE))
        mle = c_(C_LES, 128) if sample else c_(C_LE, 128)
        mgt = c_(C_GTS, 128) if sample else c_(C_GT, 128)
        for b in range(NB):
            tk = slice(b * 128, (b + 1) * 128)
            for h in range(4):
                spc = sp_tokf[:, b, h * 128:(h + 1) * 128]
                pcs = psr.next()
                P.mm(pcs[:, 0:128], spc, mle, True, True, reads=[sp_tok, cst], writes=[pcs])
                prem = psr.next()
                P.mm(prem[:, 0:128], mgt, spc, True, True, reads=[sp_tok, cst], writes=[prem])
                P.act(ge1[:, h, :], pcs[:, 0:128], AF.Exp, reads=[pcs], writes=[ge1], scale=-1.0 / 16)
                e2 = wk.next()
                P.act(e2[:, 0:128], pcs[:, 0:128], AF.Exp, reads=[pcs], writes=[e2], scale=1.0 / 16)
                e3 = wk.next()
                P.act(e3[:, 0:128], prem[:, 0:128], AF.Exp, reads=[prem], writes=[e3], scale=-1.0 / 16)
                P.tt(gq[:, h, :], qaTf[:, h, tk], ge1f[:, h, :], ALU.mult, reads=[qaT, ge1], writes=[gq])
                P.tt(gk_[:, h, :], kaTf[:, h, tk], e2[:, 0:128], ALU.mult, reads=[kaT, e2], writes=[gk_])
                P.tt(ge[:, h, :], ka_tokf[:, b, h * 128:(h + 1) * 128], e3[:, 0:128], ALU.mult,
                     reads=[ka_tok, e3], writes=[ge])
            if sample:
                for j in range(NSEQ):
                    P.dma(Ssm[:, :], sg_in[so + j], writes=[Ssm], sem=Ssm)
                    pso = psr.next()
                    for h in range(4):
                        for ec in range(2):
                            q = h * 2 + ec
                            P.mm(pso[:, q * 4:q * 4 + 4], Ssm[:, h * 256 + ec * 128: h * 256 + (ec + 1) * 128],
                                 gqf[:, h, 4 * j:4 * j + 4], True, True, reads=[Ssm, gq], writes=[pso])
                    P.copy(ost[:, :, 4 * j:4 * j + 4], pso[:, 0:32].rearrange("p (q t) -> p q t", t=4),
                           reads=[pso], writes=[ost], eng=evac_eng())
                    for h in range(4):
                        km = wk.next()
                        P.ts(km[:, 0:128], gef[:, h, :], cst[:, C_SEQ + j:C_SEQ + j + 1], ALU.mult,
                             reads=[ge, cst], writes=[km])
                        pd = psr.next()
                        P.mm(pd[:, 0:256], km[:, 0:128], va_tokf[:, 0, h * 256:(h + 1) * 256], True, True,
                             reads=[km, va_tok], writes=[pd])
                        P.stt(Ssm[:, h * 256:(h + 1) * 256], Ssm[:, h * 256:(h + 1) * 256],
                              ge1f[:, h, 4 * j + 3:4 * j + 4], pd[:, 0:256], ALU.mult, ALU.add,
                              reads=[Ssm, ge1, pd], writes=[Ssm])
                    P.dma(ss_out[so + j], Ssm[:, :], reads=[Ssm], sem=Ssm)
            for h in range(4):
                pa = psr.next()
                P.mm(pa[:, 0:128], gk_f[:, h, :], gqf[:, h, :], True, True, reads=[gk_, gq], writes=[pa])
                am = wk.next()
                P.tt(am[:, 0:128], pa[:, 0:128], mle, ALU.mult, reads=[pa, cst], writes=[am])
                for ec in range(2):
                    q = h * 2 + ec
                    po = psr.next()
                    P.mm(po[:, 0:128], va_tokf[:, b, h * 256 + ec * 128: h * 256 + (ec + 1) * 128], am[:, 0:128],
                         True, sample, reads=[va_tok, am], writes=[po])
                    if not sample:
                        P.mm(po[:, 0:128], S[:, h * 256 + ec * 128: h * 256 + (ec + 1) * 128], gqf[:, h, :],
                             False, True, reads=[S, gq], writes=[po])
                        P.copy(fT[:, q, tk], po[:, 0:128], reads=[po], writes=[fT], eng=evac_eng())
                    else:
                        P.tt(fT[:, q, 0:64], po[:, 0:64], ost[:, q, :], ALU.add, reads=[po, ost], writes=[fT])
                        P.copy(fT[:, q, 64:128], po[:, 64:128], reads=[po], writes=[fT], eng=evac_eng())
                if not sample:
                    pd = psr.next()
                    P.mm(pd[:, 0:256], gef[:, h, :], va_tokf[:, b, h * 256:(h + 1) * 256], True, True,
                         reads=[ge, va_tok], writes=[pd])
                    P.stt(S[:, h * 256:(h + 1) * 256], S[:, h * 256:(h + 1) * 256], ge1f[:, h, 127:128],
                          pd[:, 0:256], ALU.mult, ALU.add, reads=[S, ge1, pd], writes=[S])
        if last_prompt:
            P.dma(sp_out[qi * 128:(qi + 1) * 128, :], S[:, :], reads=[S], sem=S)
        P.phase = kind + ":glagate"
        rs = []
        for h in range(4):
            rs.append(rms_stats(fT, [2 * h, 2 * h + 1], T, 256, C_ZERO) if h < 2 else None)
        ra_slots = {}

        def gate_chunk(c, r):
            s, j = divmod(c, 2)
            if s not in ra_slots:
                ra_slots.clear()
                ra_slots[s] = load_slab("ra", s)
            ps = fm_group(ra_slots[s], 256, j, nT, T)
            sr = wk.next()
            P.act(sr[:, 0:T], ps[:, 0:T], AF.Silu, reads=[ps], writes=[sr])
            t1 = wk.next()
            P.stt(t1[:, 0:T], fT[:, c, 0:T], par[:, P_GLAG + (c % 2):P_GLAG + (c % 2) + 1], r[:, 0:T],
                  ALU.mult, ALU.mult, reads=[fT, par, r], writes=[t1])
            P.tt(oaT[:, c, 0:T], t1[:, 0:T], sr[:, 0:T], ALU.mult, reads=[t1, sr], writes=[oaT])

        for h in range(2):
            for ec in range(2):
                gate_chunk(2 * h + ec, rs[h])
        for h in range(2, 4):
            r = rms_stats(fT, [2 * h, 2 * h + 1], T, 256, C_ZERO)
            for ec in range(2):
                gate_chunk(2 * h + ec, r)

        P.phase = kind + ":sbproj"
        P.barrier(roleA, roleB)
        proj_fm("qb", T, lambda c: (QT[:, c, 0:T], QT), scale=0.125)
        proj_fm("kb", T, lambda c: (KTst[:, c, 0:T], KTst))
        kout = ks if sample else kp
        vout = vs if sample else vp
        kst = {}

        def k_dst(b, s):
            if s == 0:
                kst[b] = tok.next()
            return kst[b][:, s * 256:(s + 1) * 256], kst[b]

        ns_ = wshapes["kb"][0]
        for s in range(ns_):
            slot = load_slab("kb", s)
            for b in range(NB):
                ps = psr.next()
                for kc in range(8):
                    P.mm(ps[:, 0:256], nT[:, kc, b * 128:(b + 1) * 128], slot[:, kc * 256:(kc + 1) * 256],
                         kc == 0, kc == 7, reads=[nT, slot], writes=[ps])
                ap, tl = k_dst(b, s)
                P.copy(ap, ps[:, 0:256], reads=[ps], writes=[tl], eng=evac_eng())
        for b in range(NB):
            P.dma(kout[ro + t0 + b * 128:ro + t0 + (b + 1) * 128, :], kst[b][:, :], reads=[kst[b]], sem=kst[b])
        vres = Res("vrows")
        if sample:
            proj_tm("vb", T, lambda b, s: (vtok_s[:, s * 256:(s + 1) * 256], vtok_s))
            P.dma(vout[ro:ro + 128, :], vtok_s[:, :].bitcast(F32), reads=[vtok_s], sem=vtok_s)
        else:
            vst = {}

            def v_dst(b, s):
                if s == 0:
                    vst[b] = tok.next()
                return vst[b][:, s * 256:(s + 1) * 256], vst[b]

            for s in range(ns_):
                slot = load_slab("vb", s)
                for b in range(NB):
                    ps = psr.next()
                    for kc in range(8):
                        P.mm(ps[:, 0:256], nT[:, kc, b * 128:(b + 1) * 128], slot[:, kc * 256:(kc + 1) * 256],
                             kc == 0, kc == 7, reads=[nT, slot], writes=[ps])
                    ap, tl = v_dst(b, s)
                    P.copy(ap, ps[:, 0:256], reads=[ps], writes=[tl], eng=evac_eng())
            for b in range(NB):
                P.dma(vout[ro + t0 + b * 128:ro + t0 + (b + 1) * 128, :], vst[b][:, :], reads=[vst[b]], writes=[vres_all],
                      sem=vst[b])
            P.dma(kts[:, :, t0:t0 + T].rearrange("c p t -> p c t"), KTst[:, :, 0:T],
                  reads=[KTst], writes=[kres_all], sem=KTst)

        P.phase = kind + ":attn"
        if not sample:
            nkb_total = (t0 + T) // 128
            nch = (nkb_total + 3) // 4
            items = [(hp, ch) for hp in range(8) for ch in reversed(range(nch))]

            def issue(hp, ch):
                kb0 = ch * 4
                nb = min(4, nkb_total - kb0)
                kt = ktr.next()
                va, vb_ = vtr.next()
                P.dma(kt[:, 0:nb * 128], kts[hp, :, kb0 * 128:(kb0 + nb) * 128],
                      reads=[kres_all], writes=[kt], sem=kt, eng="sp")
                for e, vt in ((0, va), (1, vb_)):
                    P.dma(vt[:, 0:nb, e * 64:(e + 1) * 64],
                          vp[ro + kb0 * 128:ro + (kb0 + nb) * 128, hp * 128 + e * 64: hp * 128 + (e + 1) * 64]
                          .rearrange("(b p) d -> p b d", p=128),
                          reads=[vres_all], writes=[vt], sem=vt, eng="pool")
                return kt, va, vb_, kb0, nb

            pending = issue(*items[0])
            first = True
            for n_it, (hp, ch) in enumerate(items):
                kt, va, vb_, kb0, nb = pending
                if n_it + 1 < len(items):
                    pending = issue(*items[n_it + 1])
                po = psO[hp % 2]
                if ch == nch - 1:
                    first = True
                for bl in reversed(range(nb)):
                    kb = kb0 + bl
                    dg = kb * 128 - t0
                    for e, vt in ((0, va), (1, vb_)):
                        h = 2 * hp + e
                        rows = slice(e * 64, (e + 1) * 64)
                        pss = psr.next()
                        P.mm(pss[:, 0:T], kt[rows, bl * 128:(bl + 1) * 128], QT[rows, hp, 0:T], True, False,
                             reads=[kt, QT], writes=[pss])
                        ez = wk.next()
                        P.act(ez[:, 0:T], pss[:, 0:T], AF.Exp, reads=[pss, par], writes=[ez],
                              bias=par[:, P_SBB + h:P_SBB + h + 1])
                        Lr = wkr.next()
                        P.act(Lr[:, 0:T], ez[:, 0:T], AF.Ln, reads=[ez, cst], writes=[Lr], bias=c_(C_ONE))
                        if dg >= 0:
                            md = C_MD0 if dg == 0 else C_MD1
                            P.tt(Lr[:, 0:T], Lr[:, 0:T].bitcast(F32), cst[:, md:md + T], ALU.mult,
                                 reads=[Lr, cst], writes=[Lr], eng="dve")
                        isfirst = (kb == nkb_total - 1)
                        P.mm(pss[:, 0:T], nge_r, Lr[:, 0:T], False, isfirst, reads=[cstr, Lr], writes=[pss])
                        if not isfirst:
                            P.mm(pss[:, 0:T], nones_r, lacc[e][:, 0:T], False, True,
                                 reads=[cstr, lacc[e]], writes=[pss])
                        A = wkb.next()
                        P.act(A[:, 0:T], pss[:, 0:T], AF.Exp, reads=[pss, par], writes=[A],
                              bias=par[:, P_SBB + h:P_SBB + h + 1])
                        if dg >= 0:
                            P.tt(A[:, 0:T], A[:, 0:T], cst[:, md:md + T], ALU.mult,
                                 reads=[A, cst], writes=[A], eng="dve")
                        last = (kb == 0 and e == 1)
                        P.mm(po[:, 0:T], vt[:, bl, :], A[:, 0:T], first, last, reads=[vt, A], writes=[po])
                        first = False
                        if isfirst:
                            P.copy(lacc[e][:, 0:T], Lr[:, 0:T].bitcast(F32), reads=[Lr], writes=[lacc[e]],
                                   eng="dve")
                        elif kb > 0:
                            P.tt(lacc[e][:, 0:T], lacc[e][:, 0:T].bitcast(F32), Lr[:, 0:T].bitcast(F32), ALU.add,
                                 reads=[lacc[e], Lr], writes=[lacc[e]], eng="dve")
                if ch == 0:
                    P.copy(obT[:, hp, 0:T], po[:, 0:T], reads=[po], writes=[obT], eng=evac_eng())
        else:
            sample_attention()

        P.phase = kind + ":merge"
        mT = QT
        for s in range(4):
            s_og = load_slab("wog", s)
            pA = [fm_group(s_og, 256, j, oaT, T) for j in range(2)]
            s_ga = load_slab("ga", s)
            t1s = []
            for j in range(2):
                pGA = fm_group(s_ga, 256, j, nT, T)
                sa = wk.next()
                P.act(sa[:, 0:T], pGA[:, 0:T], AF.Sigmoid, reads=[pGA], writes=[sa])
                t1 = mrg[j]
                P.tt(t1[:, 0:T], sa[:, 0:T], pA[j][:, 0:T], ALU.mult, reads=[sa, pA[j]], writes=[t1])
                t1s.append(t1)
            s_os = load_slab("wos", s)
            pB = [fm_group(s_os, 256, j, obT, T) for j in range(2)]
            s_gb = load_slab("gb", s)
            for j in range(2):
                c = 2 * s + j
                pGB = fm_group(s_gb, 256, j, nT, T)
                sb_ = wk.next()
                P.act(sb_[:, 0:T], pGB[:, 0:T], AF.Sigmoid, reads=[pGB], writes=[sb_])
                t2 = wk.next()
                P.tt(t2[:, 0:T], sb_[:, 0:T], pB[j][:, 0:T], ALU.mult, reads=[sb_, pB[j]], writes=[t2])
                P.tt(mT[:, c, 0:T], t1s[j][:, 0:T], t2[:, 0:T], ALU.add, reads=[t1s[j], t2], writes=[mT], eng="pool")
        for s in range(4):
            slot = load_slab("wout", s)
            for j in range(2):
                c = 2 * s + j
                ps = fm_group(slot, 256, j, mT, T)
                P.copy(fT[:, c, 0:T], ps[:, 0:T], reads=[ps], writes=[fT], eng=evac_eng())
        resid_add(P_MPOST, T, C_ZERO)
        P.phase = kind + ":ffn2"
        ffn(P_F2PRE, P_F2POST, "f2g", "f2u", "f2d", T)
        P.phase = kind + ":yout"
        yout = ys if sample else yp
        for b in range(NB):
            st = tok.next()
            for half in range(2):
                ps = psr.next()
                for q in range(4):
                    c = half * 4 + q
                    P.tr(ps[:, q * 128:(q + 1) * 128], hT[:, c, b * 128:(b + 1) * 128], ident,
                         reads=[hT, cst], writes=[ps])
                P.copy(st[:, half * 512:(half + 1) * 512], ps[:, 0:512], reads=[ps], writes=[st], eng=evac_eng())
            P.dma(yout[ro + t0 + b * 128:ro + t0 + (b + 1) * 128, :], st[:, :], reads=[st], sem=st)

    vres_all = Res("vres_all")
    kres_all = Res("kres_all")
    first_blk = {}

    AR = arena

    def sample_attention():
        T = 128
        kpage_r = Res("kpage")
        vpage_r = [Res("vpage0"), Res("vpage1")]
        ktp_r = ktp_t
        qblk_r = qblk_t
        kpage = AR[:, 0:1024].bitcast(F32)
        vpage = [AR[:, 1024:2048], AR[:, 2048:3072]]
        qblk = qblk_t
        sbb64 = par[:, P_SBB64:P_SBB64 + 64]
        LA = lacc[0]
        nblk = NPG + 1
        blocks = [(j, g) for j in range(NSEQ) for g in reversed(range(nblk))]
        vpi = {}
        kdone = set()
        vcnt = [0]

        def ensure_v(n):
            j_, g_ = blocks[n]
            if g_ == NPG or n in vpi:
                return
            pi = vcnt[0] % 2
            vcnt[0] += 1
            vpi[n] = pi
            col = j_ * NPG + g_
            P.op("pool", lambda e_, col=col, pi=pi: e_.indirect_dma_start(
                out=vpage[pi], out_offset=None, in_=cv.bitcast(F32R),
                in_offset=bass.IndirectOffsetOnAxis(ap=idx[:, col:col + 1], axis=0)),
                reads=[idx], writes=[vpage_r[pi]], dma="vpage%d" % pi)

        def ensure_k(n):
            j_, g_ = blocks[n]
            if g_ == NPG or n in kdone:
                return
            kdone.add(n)
            col = j_ * NPG + g_
            P.op("pool", lambda e_, col=col: e_.indirect_dma_start(
                out=AR[:, 0:1024], out_offset=None, in_=ck.bitcast(F32R),
                in_offset=bass.IndirectOffsetOnAxis(ap=idx[:, col:col + 1], axis=0)),
                reads=[idx], writes=[kpage_r], dma="kpage")

        for n, (j, g) in enumerate(blocks):
            new = g == NPG
            isfirst = new
            par_i = n % 2
            if new:
                P.ts(qblk_t[:, :, :], cst[:, 0:512].rearrange("p (c n) -> p c n", n=64), 0.0, ALU.mult, reads=[cst],
                     writes=[qblk_r], eng="pool")
                for hp in range(8):
                    for e in range(2):
                        h = 2 * hp + e
                        rows = slice(e * 64, (e + 1) * 64)
                        P.copy(qblk[rows, hp, h * 4:h * 4 + 4], QT[rows, hp, 4 * j:4 * j + 4],
                               reads=[QT], writes=[qblk_r], eng=evac_eng())
            ensure_v(n)
            ensure_k(n)
            if n + 1 < len(blocks):
                ensure_v(n + 1)
                if new:
                    ensure_k(n + 1)
            if True:
                if new:
                    ktsrc = lambda hp: KTst[:, hp, 0:128]
                    kt_res = KTst
                    vsrc = vtok_s[:, :]
                    v_res = vtok_s
                else:
                    ktpv = ktp_t[par_i]
                    for half in range(2):
                        ps = psr.next()
                        for q in range(4):
                            hp = half * 4 + q
                            P.tr(ps[:, q * 128:(q + 1) * 128], kpage[:, hp * 128:(hp + 1) * 128], ident,
                                 reads=[kpage_r, cst], writes=[ps])
                        P.copy(ktp_t[par_i][:, half * 4:(half + 1) * 4, :],
                               ps[:, 0:512].rearrange("p (c n) -> p c n", n=128), reads=[ps],
                               writes=[ktp_r[par_i]], eng=evac_eng())
                    if n + 1 < len(blocks):
                        ensure_k(n + 1)
                    ktsrc = lambda hp, ktpv=ktpv: ktpv[:, hp, :]
                    kt_res = ktp_r[par_i]
                    vsrc = vpage[vpi[n]]
                    v_res = vpage_r[vpi[n]]
                pss = psr.next()
                for hp in range(8):
                    P.mm(pss[:, 0:64], ktsrc(hp), qblk[:, hp, :], hp == 0, hp == 7, reads=[kt_res, qblk_r],
                         writes=[pss])
                z = wk.next()
                P.tt(z[:, 0:64], pss[:, 0:64], sbb64, ALU.add, reads=[pss, par], writes=[z])
                ez = wk.next()
                P.act(ez[:, 0:64], z[:, 0:64], AF.Exp, reads=[z], writes=[ez])
                Lr = wkr.next()
                P.act(Lr[:, 0:64], ez[:, 0:64], AF.Ln, reads=[ez, cst], writes=[Lr], bias=c_(C_ONE))
                if new:
                    P.stt(Lr[:, 0:64], Lr[:, 0:64].bitcast(F32), cst[:, C_SEQ + j:C_SEQ + j + 1],
                          cst[:, C_MLT:C_MLT + 64], ALU.mult, ALU.mult, reads=[Lr, cst], writes=[Lr])
                psrr = psr.next()
                P.mm(psrr[:, 0:64], mgt_r, Lr[:, 0:64], True, isfirst, reads=[cstr, Lr], writes=[psrr])
                if not isfirst:
                    P.mm(psrr[:, 0:64], ones_r, LA[:, 0:64], False, True, reads=[cstr, LA], writes=[psrr])
                t1 = wk.next()
                P.tt(t1[:, 0:64], z[:, 0:64], Lr[:, 0:64].bitcast(F32), ALU.subtract, reads=[z, Lr], writes=[t1])
                t2 = wk.next()
                P.tt(t2[:, 0:64], t1[:, 0:64], psrr[:, 0:64], ALU.subtract, reads=[t1, psrr], writes=[t2])
                A = wkr.next()
                P.act(A[:, 0:64], t2[:, 0:64], AF.Exp, reads=[t2], writes=[A])
                if new:
                    P.stt(A[:, 0:64], A[:, 0:64].bitcast(F32), cst[:, C_SEQ + j:C_SEQ + j + 1],
                          cst[:, C_MLT:C_MLT + 64], ALU.mult, ALU.mult, reads=[A, cst], writes=[A])
                last = g == 0
                for half in range(2):
                    P.mm(psO[half][0:64, 0:512], A[:, 0:64], vsrc[:, half * 512:(half + 1) * 512], isfirst, last,
                         reads=[A, v_res], writes=[psO[half]])
                if isfirst:
                    P.copy(LA[:, 0:64], Lr[:, 0:64].bitcast(F32), reads=[Lr], writes=[LA], eng="pool")
                elif not last:
                    P.tt(LA[:, 0:64], LA[:, 0:64].bitcast(F32), Lr[:, 0:64].bitcast(F32), ALU.add,
                         reads=[LA, Lr], writes=[LA], eng="pool")
            if g != 0:
                continue
            prod = tok.next()
            for half in range(2):
                P.tt(prod[0:64, half * 512:(half + 1) * 512], psO[half][0:64, 0:512],
                     cst[0:64, C_HM + half * 512:C_HM + (half + 1) * 512], ALU.mult,
                     reads=[psO[half], cst], writes=[prod])
            osel = wk.next()
            P.op("dve", lambda e_, prod=prod, osel=osel: e_.tensor_reduce(
                osel[0:64, 0:64], prod[0:64, :].rearrange("p (h d) -> p d h", d=64), AX.X, ALU.add),
                reads=[prod], writes=[osel])
            o2 = wk.next()
            P.ts(o2[0:64, 0:64], osel[0:64, 0:64], cst[0:64, C_EVEN:C_EVEN + 1], ALU.mult, reads=[osel, cst],
                 writes=[o2])
            P.ts(o2[0:64, 64:128], osel[0:64, 0:64], cst[0:64, C_ODD:C_ODD + 1], ALU.mult, reads=[osel, cst],
                 writes=[o2])
            pst = psr.next()
            P.tr(pst[:, 0:64], o2[0:64, 0:128], cst[0:64, C_ID:C_ID + 64], reads=[o2, cst], writes=[pst])
            pv = pst[:, 0:64].rearrange("p (c e t) -> p c e t", e=2, t=4)
            P.copy(obT[0:64, :, 4 * j:4 * j + 4], pv[0:64, :, 0, :], reads=[pst], writes=[obT], eng=evac_eng())
            P.copy(obT[64:128, :, 4 * j:4 * j + 4], pv[64:128, :, 1, :], reads=[pst], writes=[obT], eng=evac_eng())
        P.ts(obT[:, :, 64:128], cst[:, 0:512].rearrange("p (b d) -> p b d", d=64), 0.0, ALU.mult, reads=[cst],
             writes=[obT], eng="pool")

    P.phase = "convert"
    convert_weights()
    for qi in range(NPR):
        for i, (t0, T) in enumerate(tiles):
            do_tile("prompt", t0, T, i == len(tiles) - 1, qi)
    for qi in range(NSG):
        do_tile("sample", 0, 128, False, qi)
    P.emit()
    return nc


_cache = {}
NCORES = 8


def kernel(x_prompt, x_sample, cache_k, cache_v, state_gla, page_table, meta_tokens,
           ffn1_pre_g, ffn1_w_gate, ffn1_w_up, ffn1_w_down, ffn1_post_g,
           mix_pre_g, w_in, w_gk2, b_gk, gla_norm_g, sb_bias, w_o_gla, w_o_sb, w_out, mix_post_g,
           ffn2_pre_g, ffn2_w_gate, ffn2_w_up, ffn2_w_down, ffn2_post_g):
    f32 = np.float32
    x_prompt = np.asarray(x_prompt, f32)
    x_sample = np.asarray(x_sample, f32)
    B, SEQ, _ = x_prompt.shape
    DB, DS, _ = x_sample.shape
    NPOOL = cache_k.shape[1]
    NPG = page_table.shape[1]
    L = N_META + SEQ
    n256 = L // 256
    rem = L - n256 * 256
    LPAD = n256 * 256 + (128 if rem else 0)
    ncores = NCORES
    while (DB // NSEQ) % ncores or (ncores < B and B % ncores):
        ncores //= 2
    NPR = max(1, B // ncores)
    NSG = DB // NSEQ // ncores
    assert DS == 4 and DB % NSEQ == 0

    win = np.asarray(w_in[0], f32)
    offs = np.cumsum([0, 512, 512, 1024, 16, 1024, 1024, 1024, 1024, 1024, 1024])
    seg = {n: win[:, offs[i]:offs[i + 1]] for i, n in
           enumerate(["qa", "ka", "va", "gl", "ra", "qb", "kb", "vb", "ga", "gb"])}
    glpad = np.zeros((D, 128), f32)
    glpad[:, :16] = seg["gl"]
    W = {
        "f1g": slabs(np.asarray(ffn1_w_gate[0], f32), 256), "f1u": slabs(np.asarray(ffn1_w_up[0], f32), 256),
        "f1d": slabs(np.asarray(ffn1_w_down[0], f32), 128),
        "qa": slabs(seg["qa"], 256), "ka": slabs(seg["ka"], 256), "va": slabs(seg["va"], 256),
        "gl": slabs(glpad, 128), "ra": slabs(seg["ra"], 256), "qb": slabs(seg["qb"], 256),
        "kb": slabs(seg["kb"], 256), "vb": slabs(seg["vb"], 256), "ga": slabs(seg["ga"], 256),
        "gb": slabs(seg["gb"], 256),
        "wog": slabs(np.asarray(w_o_gla[0], f32), 256), "wos": slabs(np.asarray(w_o_sb[0], f32), 256),
        "wout": slabs(np.asarray(w_out[0], f32), 256),
        "f2g": slabs(np.asarray(ffn2_w_gate[0], f32), 256), "f2u": slabs(np.asarray(ffn2_w_up[0], f32), 256),
        "f2d": slabs(np.asarray(ffn2_w_down[0], f32), 128),
    }
    wshapes = {n: tuple(W[n].shape) for n in WNAMES}
    cfg = dict(L=L, LPAD=LPAD, NPG=NPG, NPOOL=NPOOL, NPROMPT=NPR, NSG=NSG, wshapes=wshapes)
    key = (L, NPG, NPOOL, NPR, NSG)
    if key not in _cache:
        _cache[key] = build(cfg)
    nc = _cache[key]

    cst, cstr = make_consts(rem if rem else 128)
    par = np.zeros((128, P_TOT), f32)

    def gcol(g):
        return np.asarray(g[0], f32).reshape(8, 128).T

    par[:, P_F1PRE:P_F1PRE + 8] = gcol(ffn1_pre_g)
    par[:, P_F1POST:P_F1POST + 8] = gcol(ffn1_post_g)
    par[:, P_MPRE:P_MPRE + 8] = gcol(mix_pre_g)
    par[:, P_MPOST:P_MPOST + 8] = gcol(mix_post_g)
    par[:, P_F2PRE:P_F2PRE + 8] = gcol(ffn2_pre_g)
    par[:, P_F2POST:P_F2POST + 8] = gcol(ffn2_post_g)
    par[:, P_GLAG:P_GLAG + 2] = np.asarray(gla_norm_g[0], f32).reshape(2, 128).T
    sbv = np.asarray(sb_bias[0], f32)
    par[:, P_SBB:P_SBB + 16] = np.broadcast_to(sbv[None, :], (128, 16))
    par[:, P_SBB64:P_SBB64 + 64] = np.broadcast_to(np.repeat(sbv, 4)[None, :], (128, 64))

    ckf = np.asarray(cache_k[0], f32).reshape(NPOOL * 128, D)
    cvf = np.asarray(cache_v[0], f32).reshape(NPOOL * 128, D)
    meta = np.asarray(meta_tokens, f32)
    pt = np.asarray(page_table, np.int32)
    sg = np.asarray(state_gla[0], f32)
    in_maps = []
    for c in range(ncores):
        xpc = np.zeros((NPR * LPAD, D), f32)
        for i in range(NPR):
            xpc[i * LPAD:i * LPAD + N_META] = meta
            xpc[i * LPAD + N_META:i * LPAD + L] = x_prompt[(c * NPR + i) % B]
        xsc = np.zeros((NSG, 128, D), f32)
        s0 = c * NSG * NSEQ
        xsc[:, :NSEQ * 4] = x_sample[s0:s0 + NSG * NSEQ].reshape(NSG, NSEQ * 4, D)
        m = dict(xp=xpc, xs=xsc.reshape(NSG * 128, D), ck=ckf, cv=cvf,
                 ptb=np.ascontiguousarray(pt[s0:s0 + NSG * NSEQ].reshape(NSG, NSEQ * NPG)),
                 sg=np.ascontiguousarray(sg[s0:s0 + NSG * NSEQ].transpose(0, 2, 1, 3)).reshape(NSG * NSEQ, 128, 1024),
                 cst=cst, cstr=cstr, par=par,
                 wgk=np.asarray(w_gk2[0], f32), bgk=np.asarray(b_gk, f32).reshape(1, 512))
        for n in WNAMES:
            m["w_" + n] = W[n]
        in_maps.append(m)
    res = run_bass_kernel_spmd(nc, in_maps, core_ids=list(range(ncores)))
    R = res.results

    npc = B // NPR

    def prow(name):
        return np.concatenate([R[c][name].reshape(NPR, LPAD, D) for c in range(npc)])

    def srow(name):
        return np.concatenate([R[c][name].reshape(NSG, 128, D)[:, :NSEQ * 4].reshape(NSG * NSEQ, 4, D)
                               for c in range(ncores)])

    y_prompt = prow("yp")[:, N_META:L]
    k_prompt = prow("kp")[:, :L].reshape(B, L, 16, 64)[None]
    v_prompt = prow("vp")[:, :L].reshape(B, L, 16, 64)[None]
    gsp = np.concatenate([R[c]["spo"].reshape(NPR, 128, 4, 256) for c in range(npc)]).transpose(0, 2, 1, 3)[None]
    y_sample = srow("ys")
    k_sample = srow("ks").reshape(DB, 4, 16, 64)[None]
    v_sample = srow("vs").reshape(DB, 4, 16, 64)[None]
    gss = np.concatenate([R[c]["sso"].reshape(NSG * NSEQ, 128, 4, 256) for c in range(ncores)]).transpose(0, 2, 1, 3)[None]
    return (np.ascontiguousarray(y_prompt, f32), np.ascontiguousarray(y_sample, f32),
            np.ascontiguousarray(k_prompt, f32), np.ascontiguousarray(v_prompt, f32),
            np.ascontiguousarray(gsp, f32), np.ascontiguousarray(k_sample, f32),
            np.ascontiguousarray(v_sample, f32), np.ascontiguousarray(gss, f32))
```
